# Optimizing a Trainium2 kernel written in Bass

```python
import math
import jax, jax.numpy as jnp
from jax import lax
import numpy as np

D_MODEL = 1024
BATCH = 4
SEQ = 4096
DEPTH = 2
DEC_BATCH = 16
DEC_SEQ = 16
PAST_LEN = 4096

CHUNK = 64
N_A = DEPTH // 2
N_B = DEPTH - N_A
N_MEM = 256
DA_HEADS = 4
DA_QK = 64
DA_V = 2 * DA_QK
DA_W = DA_HEADS * DA_V
SW_HEADS = 8
SW_KV = 2
SW_GROUP = SW_HEADS // SW_KV
SW_HD = 64
SW_W = SW_HEADS * SW_HD
WINDOW = 128
MEM_HEADS = 4
MEM_HD = 128
MEM_W = MEM_HEADS * MEM_HD
MIX_W = DA_W + MEM_W
A_SIZES = (DA_HEADS * 2 * DA_QK, DA_HEADS * 2 * DA_QK, DA_W, DA_W, MEM_W, MEM_W)
B_SIZES = (SW_W, SW_W, MEM_W, MEM_W)
ROPE_THETA = 500000.0
ROT_FRAC = 4
Q_BLOCK = 128
DN_ALPHA = (2 * DEPTH) ** 0.25
DN_BETA = (8 * DEPTH) ** -0.25
LN_EPS = 1e-5
NEG = -1e30

kernel_name = "yoco_diffattn_swa_sink_stream_step"


def _split(h, sizes):
    return jnp.split(h, [int(c) for c in np.cumsum(sizes)[:-1]], axis=-1)


def rope_tables(pos):
    half = SW_HD // ROT_FRAC // 2
    inv = ROPE_THETA ** (-jnp.arange(half, dtype=jnp.float32) / half)
    ang = pos.astype(jnp.float32)[:, None] * inv[None, :]
    return jnp.cos(ang), jnp.sin(ang)


def apply_partial_rope(x, cos, sin):
    r = x.shape[-1] // ROT_FRAC
    half = r // 2
    shape = (1, cos.shape[0]) + (1,) * (x.ndim - 3) + (half,)
    c = cos.reshape(shape).astype(x.dtype)
    s = sin.reshape(shape).astype(x.dtype)
    x1, x2, xp = x[..., :half], x[..., half:r], x[..., r:]
    return jnp.concatenate([x1 * c - x2 * s, x2 * c + x1 * s, xp], axis=-1)


def layer_norm(x, g, b):
    xf = x.astype(jnp.float32)
    mu = jnp.mean(xf, -1, keepdims=True)
    var = jnp.mean(jnp.square(xf - mu), -1, keepdims=True)
    return ((xf - mu) * lax.rsqrt(var + LN_EPS) * g.astype(jnp.float32) + b.astype(jnp.float32)).astype(x.dtype)


def head_rms(o, g):
    of = o.astype(jnp.float32)
    return (of * lax.rsqrt(jnp.mean(of * of, -1, keepdims=True) + LN_EPS) * g.astype(jnp.float32)).astype(o.dtype)


def mem_kv(mem, w):
    B = mem.shape[0]
    k, v = jnp.split(mem @ w, 2, axis=-1)
    return k.reshape(B, N_MEM, MEM_HEADS, MEM_HD), v.reshape(B, N_MEM, MEM_HEADS, MEM_HD)


def mem_attend(q, mk, mv):
    s = jnp.einsum('bqhd,bkhd->bhqk', q, mk).astype(jnp.float32) * (MEM_HD ** -0.5)
    p = jax.nn.softmax(s, axis=-1)
    o = jnp.einsum('bhqk,bkhd->bqhd', p.astype(mv.dtype), mv)
    return o.reshape(q.shape[0], q.shape[1], MEM_W)


def diff_attend(q, k, v, mask, lam):
    s = jnp.einsum('bqhcd,bkhcd->bhcqk', q, k).astype(jnp.float32) * (DA_QK ** -0.5)
    if mask is not None:
        s = jnp.where(mask, s, NEG)
    p = jax.nn.softmax(s, axis=-1)
    a = p[:, :, 0] - lam * p[:, :, 1]
    return jnp.einsum('bhqk,bkhd->bqhd', a.astype(v.dtype), v)


def diff_prompt(q, k, v, lam):
    B, S = q.shape[:2]
    nb = S // Q_BLOCK
    qb = q.reshape((B, nb, Q_BLOCK) + q.shape[2:]).swapaxes(0, 1)
    kchunk = jnp.arange(S) // CHUNK

    def one(args):
        qi, i = args
        qchunk = (i * Q_BLOCK + jnp.arange(Q_BLOCK)) // CHUNK
        return diff_attend(qi, k, v, kchunk[None, :] <= qchunk[:, None], lam)

    out = lax.map(one, (qb, jnp.arange(nb)))
    return out.swapaxes(0, 1).reshape(B, S, DA_HEADS, DA_V)


def sink_softmax(s, sinks):
    sk = sinks.astype(jnp.float32).reshape(SW_KV, SW_GROUP)[:, :, None, None]
    sk = jnp.broadcast_to(sk, s.shape[:-1] + (1,))
    return jax.nn.softmax(jnp.concatenate([s, sk], axis=-1), axis=-1)[..., :-1]


def swa_prompt(q, k, v, sinks):
    B, S = q.shape[:2]
    nc = S // CHUNK
    nw = WINDOW // CHUNK
    qc = q.reshape(B, nc, CHUNK, SW_KV, SW_GROUP, SW_HD)
    pad = ((0, 0), (WINDOW, 0), (0, 0), (0, 0))
    kc = jnp.pad(k, pad).reshape(B, nc + nw, CHUNK, SW_KV, SW_HD)
    vc = jnp.pad(v, pad).reshape(B, nc + nw, CHUNK, SW_KV, SW_HD)
    kb = jnp.concatenate([kc[:, j:j + nc] for j in range(nw + 1)], axis=2)
    vb = jnp.concatenate([vc[:, j:j + nc] for j in range(nw + 1)], axis=2)
    s = jnp.einsum('bcqkgd,bcskd->bckgqs', qc, kb).astype(jnp.float32) * (SW_HD ** -0.5)
    blk = jnp.repeat(jnp.arange(nw + 1), CHUNK)[None, :]
    valid = (jnp.arange(nc)[:, None] + blk) >= nw
    s = jnp.where(valid[None, :, None, None, None, :], s, NEG)
    p = sink_softmax(s, sinks)
    o = jnp.einsum('bckgqs,bcskd->bcqkgd', p.astype(v.dtype), vb)
    return o.reshape(B, S, SW_W)


def swa_sample(q, k, v, sinks):
    B, T = q.shape[:2]
    s = jnp.einsum('bqkgd,bskd->bkgqs', q, k).astype(jnp.float32) * (SW_HD ** -0.5)
    p = sink_softmax(s, sinks)
    o = jnp.einsum('bkgqs,bskd->bqkgd', p.astype(v.dtype), v)
    return o.reshape(B, T, SW_W)


def a_project(x, w_in, cos, sin):
    B, T, _ = x.shape
    q, k, v, gd, mq, gm = _split(x @ w_in, A_SIZES)
    q = apply_partial_rope(q.reshape(B, T, DA_HEADS, 2, DA_QK), cos, sin)
    k = apply_partial_rope(k.reshape(B, T, DA_HEADS, 2, DA_QK), cos, sin)
    return q, k, v.reshape(B, T, DA_HEADS, DA_V), gd, mq.reshape(B, T, MEM_HEADS, MEM_HD), gm


def b_project(x, w_in, cos, sin):
    B, T, _ = x.shape
    q, gs, mq, gm = _split(x @ w_in, B_SIZES)
    q = apply_partial_rope(q.reshape(B, T, SW_KV, SW_GROUP, SW_HD), cos, sin)
    return q, gs, mq.reshape(B, T, MEM_HEADS, MEM_HD), gm


def shared_kv(h, w_kv, cos, sin):
    B, T, _ = h.shape
    k, v = jnp.split(h @ w_kv, 2, axis=-1)
    k = apply_partial_rope(k.reshape(B, T, SW_KV, SW_HD), cos, sin)
    return k, v.reshape(B, T, SW_KV, SW_HD)


def diff_finish(od, g, lam_init):
    B, T = od.shape[:2]
    return (head_rms(od, g) * (1.0 - lam_init)).reshape(B, T, DA_W)


def merge(x, o1, g1, o2, g2, w_out, ln_g, ln_b):
    mix = jnp.concatenate([o1 * jax.nn.silu(g1), o2 * jax.nn.silu(g2)], axis=-1)
    return layer_norm(DN_ALPHA * x + mix @ w_out, ln_g, ln_b)


def setup_inputs(seed: int = 0) -> dict:
    key = jax.random.key(seed)
    ks = jax.random.split(key, 24)

    def nrm(k, shape, scale=1.0):
        return jax.random.normal(k, shape, jnp.float32) * scale

    wr = min(WINDOW, PAST_LEN)
    return {
        "x_prompt": nrm(ks[0], (BATCH, SEQ, D_MODEL)),
        "x_sample": nrm(ks[1], (DEC_BATCH, DEC_SEQ, D_MODEL)),
        "mem_prompt": nrm(ks[2], (BATCH, N_MEM, D_MODEL)),
        "cache_diff_k": nrm(ks[3], (N_A, DEC_BATCH, PAST_LEN, DA_HEADS, 2, DA_QK)),
        "cache_diff_v": nrm(ks[4], (N_A, DEC_BATCH, PAST_LEN, DA_HEADS, DA_V)),
        "cache_swa_k": nrm(ks[5], (DEC_BATCH, wr, SW_KV, SW_HD)),
        "cache_swa_v": nrm(ks[6], (DEC_BATCH, wr, SW_KV, SW_HD)),
        "cache_mem_k": nrm(ks[7], (DEPTH, DEC_BATCH, N_MEM, MEM_HEADS, MEM_HD)),
        "cache_mem_v": nrm(ks[8], (DEPTH, DEC_BATCH, N_MEM, MEM_HEADS, MEM_HD)),
        "w_in_a": nrm(ks[9], (N_A, D_MODEL, sum(A_SIZES)), D_MODEL ** -0.5),
        "lam_q1": nrm(ks[10], (N_A, DA_QK), 0.1),
        "lam_k1": nrm(ks[11], (N_A, DA_QK), 0.1),
        "lam_q2": nrm(ks[12], (N_A, DA_QK), 0.1),
        "lam_k2": nrm(ks[13], (N_A, DA_QK), 0.1),
        "diff_norm_g": 1.0 + nrm(ks[14], (N_A, DA_V), 0.02),
        "w_in_b": nrm(ks[15], (N_B, D_MODEL, sum(B_SIZES)), D_MODEL ** -0.5),
        "sinks": nrm(ks[16], (N_B, SW_HEADS), 0.5),
        "w_kv_shared": nrm(ks[17], (D_MODEL, 2 * SW_KV * SW_HD), D_MODEL ** -0.5),
        "w_mem_kv": nrm(ks[18], (DEPTH, D_MODEL, 2 * MEM_W), D_MODEL ** -0.5),
        "w_out": nrm(ks[19], (DEPTH, MIX_W, D_MODEL), MIX_W ** -0.5 * DN_BETA),
        "ln_g": 1.0 + nrm(ks[20], (DEPTH, D_MODEL), 0.02),
        "ln_b": nrm(ks[21], (DEPTH, D_MODEL), 0.02),
    }


def reference(x_prompt, x_sample, mem_prompt, cache_diff_k, cache_diff_v, cache_swa_k, cache_swa_v,
              cache_mem_k, cache_mem_v, w_in_a, lam_q1, lam_k1, lam_q2, lam_k2, diff_norm_g,
              w_in_b, sinks, w_kv_shared, w_mem_kv, w_out, ln_g, ln_b):
    S = x_prompt.shape[1]
    T = x_sample.shape[1]
    P = cache_diff_k.shape[2]
    cos_p, sin_p = rope_tables(jnp.arange(S))
    cos_s, sin_s = rope_tables(P + jnp.arange(T))
    xp, xs = x_prompt, x_sample
    dkp, dvp, dks, dvs, mkp, mvp = [], [], [], [], [], []
    for l in range(DEPTH):
        mk_p, mv_p = mem_kv(mem_prompt, w_mem_kv[l])
        mkp.append(mk_p)
        mvp.append(mv_p)
        mk_s, mv_s = cache_mem_k[l], cache_mem_v[l]
        if l < N_A:
            lam_init = 0.8 - 0.6 * math.exp(-0.3 * l)
            lam = (jnp.exp(jnp.sum(lam_q1[l].astype(jnp.float32) * lam_k1[l].astype(jnp.float32)))
                   - jnp.exp(jnp.sum(lam_q2[l].astype(jnp.float32) * lam_k2[l].astype(jnp.float32)))
                   + lam_init)
            q, k, v, gd, mq, gm = a_project(xp, w_in_a[l], cos_p, sin_p)
            od = diff_finish(diff_prompt(q, k, v, lam), diff_norm_g[l], lam_init)
            om = mem_attend(mq, mk_p, mv_p)
            xp = merge(xp, od, gd, om, gm, w_out[l], ln_g[l], ln_b[l])
            dkp.append(k)
            dvp.append(v)
            q, k, v, gd, mq, gm = a_project(xs, w_in_a[l], cos_s, sin_s)
            kk = jnp.concatenate([cache_diff_k[l].astype(k.dtype), k], axis=1)
            vv = jnp.concatenate([cache_diff_v[l].astype(v.dtype), v], axis=1)
            od = diff_finish(diff_attend(q, kk, vv, None, lam), diff_norm_g[l], lam_init)
            om = mem_attend(mq, mk_s, mv_s)
            xs = merge(xs, od, gd, om, gm, w_out[l], ln_g[l], ln_b[l])
            dks.append(k)
            dvs.append(v)
        else:
            if l == N_A:
                skp, svp = shared_kv(xp, w_kv_shared, cos_p, sin_p)
                skn, svn = shared_kv(xs, w_kv_shared, cos_s, sin_s)
                sks = jnp.concatenate([cache_swa_k.astype(skn.dtype), skn], axis=1)
                svs = jnp.concatenate([cache_swa_v.astype(svn.dtype), svn], axis=1)
                wr_p = min(WINDOW, S)
                wr_s = cache_swa_k.shape[1]
                swa_kp, swa_vp = skp[:, S - wr_p:], svp[:, S - wr_p:]
                swa_ks, swa_vs = sks[:, sks.shape[1] - wr_s:], svs[:, svs.shape[1] - wr_s:]
            ib = l - N_A
            q, gs, mq, gm = b_project(xp, w_in_b[ib], cos_p, sin_p)
            os_ = swa_prompt(q, skp, svp, sinks[ib])
            om = mem_attend(mq, mk_p, mv_p)
            xp = merge(xp, os_, gs, om, gm, w_out[l], ln_g[l], ln_b[l])
            q, gs, mq, gm = b_project(xs, w_in_b[ib], cos_s, sin_s)
            os_ = swa_sample(q, sks, svs, sinks[ib])
            om = mem_attend(mq, mk_s, mv_s)
            xs = merge(xs, os_, gs, om, gm, w_out[l], ln_g[l], ln_b[l])
    diff_k_prompt = jnp.stack(dkp)
    diff_v_prompt = jnp.stack(dvp)
    diff_k_sample = jnp.stack(dks)
    diff_v_sample = jnp.stack(dvs)
    mem_k_prompt = jnp.stack(mkp)
    mem_v_prompt = jnp.stack(mvp)
    return (xp, xs, diff_k_prompt, diff_v_prompt, diff_k_sample, diff_v_sample,
            swa_kp, swa_vp, swa_ks, swa_vs, mem_k_prompt, mem_v_prompt)
```

```python
import math
import numpy as np
import ml_dtypes
import concourse.bass as bass
import concourse.mybir as mybir
from concourse.bass_utils import run_bass_kernel_spmd

F32 = mybir.dt.float32
BF16 = mybir.dt.bfloat16
AF = mybir.ActivationFunctionType
ALU = mybir.AluOpType
AX = mybir.AxisListType

N_DMA_SEMS = 24
NT = 32
OWN0 = 16
T0 = 15
TS = 64
LN_EPS = 1e-5
ALPHA = 4.0 ** 0.25
LAM_INIT = 0.2


class Emitter:
    COMPUTE = ("pe", "act", "dve", "pool")

    def __init__(self, nc):
        self.nc = nc
        self.engs = {"pe": nc.tensor, "act": nc.scalar, "dve": nc.vector,
                     "pool": nc.gpsimd, "sp": nc.sync}
        self.esem = {e: nc.alloc_semaphore(name=f"es_{e}") for e in self.COMPUTE}
        self.dsem = [nc.alloc_semaphore(name=f"ds_{i}") for i in range(N_DMA_SEMS)]
        self.ops = []
        self.state = {}
        self.dma_last = [None] * N_DMA_SEMS
        self.dma_rr = 0
        self.dma_rr_sw = 0

    def _deps(self, reads, writes, eng=None):
        deps = set()
        for r in reads:
            st = self.state.get(r)
            if st and st[0] is not None:
                deps.add(st[0])
            if st and r in self.PSUM_RES:
                for e2, oid in st[1].items():
                    if e2 != eng:
                        deps.add(oid)
        for w in writes:
            st = self.state.get(w)
            if st:
                if st[0] is not None:
                    deps.add(st[0])
                deps.update(st[1].values())
        return deps

    def _record(self, oid, eng, reads, writes):
        for r in reads:
            st = self.state.setdefault(r, [None, {}])
            st[1][eng] = oid
        for w in writes:
            self.state[w] = [oid, {}]

    PSUM_RES = frozenset(["A0", "A1", "B0", "B1", "O0", "O1", "P0", "P1"])

    def op(self, eng, fn, reads=(), writes=()):
        deps = self._deps(reads, writes, eng)
        oid = len(self.ops)
        self.ops.append(dict(eng=eng, fn=fn, deps=deps, dma=None))
        self._record(oid, eng, reads, writes)
        return oid

    def dma(self, out, in_, reads=(), writes=(), eng="sp"):
        deps = self._deps(reads, writes)
        half = N_DMA_SEMS // 2
        if eng == "pool":
            k = half + self.dma_rr_sw
            self.dma_rr_sw = (self.dma_rr_sw + 1) % (N_DMA_SEMS - half)
        else:
            k = self.dma_rr
            self.dma_rr = (self.dma_rr + 1) % half
        if self.dma_last[k] is not None:
            deps.add(self.dma_last[k])
        oid = len(self.ops)
        self.ops.append(dict(eng=eng, fn=lambda e: e.dma_start(out=out, in_=in_),
                             deps=deps, dma=k))
        self.dma_last[k] = oid
        self._record(oid, ("dma", oid), reads, writes)
        return oid

    def finalize(self):
        ops = self.ops

        def skip(p, o):
            return p["dma"] is None and o["dma"] is None and p["eng"] == "pe" and o["eng"] == "pe"

        needed = [False] * len(ops)
        for o in ops:
            for d in o["deps"]:
                if not skip(ops[d], o):
                    needed[d] = True
        cnt = {e: 0 for e in self.COMPUTE}
        dcnt = [0] * N_DMA_SEMS
        val = [None] * len(ops)
        waited = {}
        for i, o in enumerate(ops):
            eng = o["eng"]
            e = self.engs[eng]
            want = {}
            for d in o["deps"]:
                if skip(ops[d], o):
                    continue
                key, v = val[d]
                if want.get(key, 0) < v:
                    want[key] = v
            for key, v in want.items():
                if waited.get((eng, key), 0) >= v:
                    continue
                waited[(eng, key)] = v
                sem = self.esem[key] if isinstance(key, str) else self.dsem[key]
                e.wait_ge(sem, v)
            ins = o["fn"](e)
            if o["dma"] is not None:
                k = o["dma"]
                dcnt[k] += 16
                ins.then_inc(self.dsem[k], 16)
                val[i] = (k, dcnt[k])
            elif needed[i]:
                cnt[eng] += 1
                ins.then_inc(self.esem[eng], 1)
                val[i] = (eng, cnt[eng])
        sp = self.engs["sp"]
        for k in range(N_DMA_SEMS):
            if dcnt[k] and waited.get(("sp", k), 0) < dcnt[k]:
                sp.wait_ge(self.dsem[k], dcnt[k])
        return dict(n_ops=len(ops), cnt=cnt)


class Rot:
    def __init__(self, nc, name, shape, dtype, n=1):
        self.bufs = [nc.alloc_sbuf_tensor(f"r_{name}{i}", shape, dtype) for i in range(n)]
        self.names = [f"{name}{i}" for i in range(n)]
        self.i = -1

    def next(self):
        self.i = (self.i + 1) % len(self.bufs)
        return self.bufs[self.i], self.names[self.i]


def build_program():
    nc = bass.Bass("TRN2", target_bir_lowering=False)
    em = Emitter(nc)

    def din(name, shape, dt=F32):
        return nc.dram_tensor(name, list(shape), dt, kind="ExternalInput").ap()

    def dout(name, shape):
        return nc.dram_tensor(name, list(shape), F32, kind="ExternalOutput").ap()

    xloc = din("xloc", [NT * 128, 1024])
    xs_d = din("xs", [TS, 1024])
    memp = din("memp", [256, 1024])
    cdk = din("cdk", [2, 4096, 512])
    cdv = din("cdv", [2, 4096, 512])
    cswk = din("cswk", [2, 128, 128])
    cswv = din("cswv", [2, 128, 128])
    cmk = din("cmk", [2, 2, 256, 512])
    cmv = din("cmv", [2, 2, 256, 512])
    w_in_a = din("w_in_a", [1024, 3072])
    w_in_b = din("w_in_b", [1024, 2048])
    w_kv = din("w_kv", [1024, 256])
    w_mem = din("w_mem", [2, 1024, 1024])
    w_out = din("w_out", [2, 1024, 1024])
    lamv = din("lamv", [4, 64])
    dng = din("dng", [1, 128])
    sinks_d = din("sinks", [1, 8])
    lng = din("lng", [2, 1024])
    lnb = din("lnb", [2, 1024])
    cosP_d = din("cosP", [128, NT, 8])
    sinP_d = din("sinP", [128, NT, 8])
    cosS_d = din("cosS", [TS, 8])
    sinS_d = din("sinS", [TS, 8])
    pflag_d = din("pflag", [128, 1])
    ident_d = din("ident", [128, 128], BF16)

    y_p = dout("y_p", [2048, 1024])
    dk_p = dout("dk_p", [2048, 512])
    dv_p = dout("dv_p", [2048, 512])
    y_s = dout("y_s", [TS, 1024])
    dk_s = dout("dk_s", [TS, 512])
    dv_s = dout("dv_s", [TS, 512])
    swk_p = dout("swk_p", [128, 128])
    swv_p = dout("swv_p", [128, 128])
    swk_s = dout("swk_s", [2, 128, 128])
    swv_s = dout("swv_s", [2, 128, 128])
    mk_p = dout("mk_p", [2, 256, 512])
    mv_p = dout("mv_p", [2, 256, 512])
    scr = nc.dram_tensor("scr", [(NT - T0) * 128, 1024], F32, kind="Internal").ap()

    def sb(n, shp, dt):
        return nc.alloc_sbuf_tensor("s_" + n, shp, dt)
    WR = sb("WR", [128, 8, 4096], BF16)
    KT = sb("KT", [128, 4, (NT + 1) * 128], BF16)
    VA = sb("VA", [128, NT + 1, 4, 129], BF16)
    _lng = sb("lngt", [128, 1024], F32)
    _lnb = sb("lnbt", [128, 1024], F32)
    lngt = [_lng, _lng]
    lnbt = [_lnb, _lnb]
    ident = sb("ident", [128, 128], BF16)
    cosP = sb("cosP", [128, NT, 8], F32)
    sinP = sb("sinP", [128, NT, 8], F32)
    cosS = sb("cosS", [128, 8], F32)
    sinS = sb("sinS", [128, 8], F32)
    pflag = sb("pflag", [128, 1], F32)
    lams = sb("lams", [128, 8], F32)
    gtab = sb("gtab", [128, 128], F32)
    esink = sb("esink", [128, 8], F32)
    cneg = sb("cneg", [128, 8], F32)
    mkTs = sb("mkTs", [128, 4, 256], BF16)
    mvas = sb("mvas", [128, 2, 4, 129], BF16)
    _mkT = sb("mkT0", [128, 4, 256], BF16)
    _mva = sb("mva0", [128, 2, 4, 129], BF16)
    mkT = [_mkT, mkTs]
    mva = [_mva, mvas]
    mkT1_scr = nc.dram_tensor("mkT1_scr", [128, 1024], BF16, kind="Internal").ap()
    mva1_scr = nc.dram_tensor("mva1_scr", [128, 1032], BF16, kind="Internal").ap()

    xbR = Rot(nc, "xb", [128, 1024], BF16, 2)
    xTR = Rot(nc, "xT", [128, 8, 128], BF16, 2)
    hR = Rot(nc, "hb", [128, 1024], F32, 2)
    f512R = Rot(nc, "f5", [128, 512], F32, 2)
    sgR = Rot(nc, "sg", [128, 512], F32, 4)
    b512R = Rot(nc, "b5", [128, 512], BF16, 2)
    qTR = Rot(nc, "qT", [128, 4, 128], BF16, 4)
    ER = Rot(nc, "E", [128, 2, 4, 128], BF16, 2)
    mixR = Rot(nc, "mix", [128, 1024], BF16, 1)
    mixTR = xTR
    OsbR = Rot(nc, "Osb", [128, 4, 2, 129], F32, 1)
    OmsbR = Rot(nc, "Omsb", [128, 4, 129], F32, 1)
    w512R = Rot(nc, "w5", [128, 512], F32, 2)
    smR = Rot(nc, "sm", [128, 32], F32, 4)
    ropeR = Rot(nc, "rp", [128, 4, 64], F32, 2)
    memb = KT[:, 0, 0:2048].rearrange("p (a b) -> p a b", a=2)
    memT = KT[:, 1, 0:2048].rearrange("p (a b) -> p a b", a=8)
    xs1 = sb("xs1", [128, 1024], F32)
    mixs = sb("mixs", [128, 1024], BF16)
    mixPR = Rot.__new__(Rot)
    mixPR.bufs = [mixR.bufs[0], mixs]
    mixPR.names = [mixR.names[0], "mixs"]
    mixPR.i = -1
    KT0 = KT[:, 0, :]
    _off = [0]

    def carve(n, shape3):
        a = KT0[:, _off[0]:_off[0] + n].rearrange("p (a b) -> p a b", a=shape3[0])
        _off[0] += n
        return a

    class RotV:
        def __init__(self, name, n, cnt, shape3):
            self.bufs = [carve(cnt, shape3) for _ in range(n)]
            self.names = [f"v_{name}{i}" for i in range(n)]
            self.i = -1

        def next(self):
            self.i = (self.i + 1) % len(self.bufs)
            return self.bufs[self.i], self.names[self.i]

    K2R = RotV("K2", 5, 256, (2, 128))
    VsR = RotV("Vs", 5, 130, (2, 65))
    kdupR = RotV("kdup", 3, 256, (2, 128))
    K2C = carve(256, (2, 128))
    cswb = carve(256, (2, 128))
    VmD = [carve(130, (2, 65)) for _ in range(2)]
    KT1 = KT[:, 1, :]
    KT2 = KT[:, 2, :]

    class RotX:
        def __init__(self, base, name, n):
            self.bufs = [base[:, i * 1024:(i + 1) * 1024].rearrange("p (a b) -> p a b", a=8) for i in range(n)]
            self.names = [f"v_{name}{i}" for i in range(n)]
            self.i = -1

        def next(self):
            self.i = (self.i + 1) % len(self.bufs)
            return self.bufs[self.i], self.names[self.i]

    xT2R = RotX(KT1, "xT2", 4)
    mixT2R = RotX(KT2, "mixT2", 2)
    L1V_NAMES = K2R.names + VsR.names + kdupR.names + ["K2C", "cswb", "VmD0", "VmD1"] + xT2R.names + mixT2R.names

    psA = nc.alloc_psum_tensor("psA", [128, 1024], F32)
    psB = nc.alloc_psum_tensor("psB", [128, 1024], F32)
    psO = nc.alloc_psum_tensor("psO", [128, 1024], F32)
    psP = [nc.alloc_psum_tensor(f"psP{i}", [128, 512], F32) for i in range(2)]
    Ssets = [(psA, ("A0", "A1")), (psB, ("B0", "B1"))]
    sidx = [0]

    S_SINGLE = [False]

    def next_S():
        if S_SINGLE[0]:
            return Ssets[0]
        sidx[0] ^= 1
        return Ssets[sidx[0]]

    pidx = [0]
    P_SMALL = [(psP[0], "P0"), (psP[1], "P1")]
    P_ALL = P_SMALL + [(psA[:, 0:512], "A0"), (psA[:, 512:1024], "A1"), (psB[:, 0:512], "B0"),
                       (psB[:, 512:1024], "B1"), (psO[:, 0:512], "O0"), (psO[:, 512:1024], "O1")]
    P_POOL = [P_ALL]

    def next_P():
        pidx[0] = (pidx[0] + 1) % len(P_POOL[0])
        return P_POOL[0][pidx[0]]

    ORES = ("O0", "O1")

    def WRr(c0, c1):
        return [("WR", b) for b in range(c0 // 512, (c1 + 511) // 512)]

    em.dma(ident[:], ident_d, writes=["ident"])
    em.dma(cosP[:], cosP_d, writes=["cosP"])
    em.dma(sinP[:], sinP_d, writes=["sinP"])
    em.dma(cosS[0:TS, :], cosS_d, writes=["cosS"])
    em.dma(sinS[0:TS, :], sinS_d, writes=["sinS"])
    em.dma(pflag[:], pflag_d, writes=["pflag"])
    lamt_b, lamt_n = f512R.next()
    lamt = lamt_b[:, 0:256].rearrange("p (a b) -> p a b", a=4)
    em.dma(lamt_b[:, 0:256], lamv.rearrange("a b -> (a b)").partition_broadcast(128), writes=[lamt_n])
    em.dma(gtab[:, :], dng[0, :].partition_broadcast(128), writes=["gtab"])
    em.dma(esink[:], sinks_d[0, :].partition_broadcast(128), writes=["esink"])
    def load_ln(l):
        em.dma(lngt[l][:], lng[l, :].partition_broadcast(128), writes=["lng"])
        em.dma(lnbt[l][:], lnb[l, :].partition_broadcast(128), writes=["lnb"])
    load_ln(0)
    em.op("dve", lambda e: e.memset(cneg[:], -0.5), writes=["cneg"])
    w, wn = w512R.next()
    em.op("dve", lambda e: e.tensor_tensor(w[:, 0:64], lamt[:, 0, :], lamt[:, 1, :], ALU.mult), reads=[lamt_n], writes=[wn])
    em.op("dve", lambda e: e.tensor_tensor(w[:, 64:128], lamt[:, 2, :], lamt[:, 3, :], ALU.mult), reads=[lamt_n, wn], writes=[wn])
    em.op("dve", lambda e: e.reduce_sum(lams[:, 4:6], w[:, 0:128].rearrange("p (a b) -> p a b", a=2), axis=AX.X), reads=[wn], writes=["lams"])
    em.op("act", lambda e: e.activation(lams[:, 0:2], lams[:, 4:6], AF.Exp), reads=["lams"], writes=["lams"])
    em.op("dve", lambda e: e.tensor_tensor(lams[:, 2:3], lams[:, 0:1], lams[:, 1:2], ALU.subtract), reads=["lams"], writes=["lams"])
    em.op("dve", lambda e: e.tensor_scalar(lams[:, 3:4], lams[:, 2:3], -1.0, -LAM_INIT, ALU.mult, ALU.add), reads=["lams"], writes=["lams"])
    neg_lam = lams[:, 3:4]
    em.op("dve", lambda e: e.tensor_scalar(gtab[:, :], gtab[:, :], (1.0 - LAM_INIT) * 0.5, None, ALU.mult), reads=["gtab"], writes=["gtab"])
    em.op("act", lambda e: e.activation(esink[:], esink[:], AF.Exp), reads=["esink"], writes=["esink"])

    CP = ["act"]
    MIXENG = ["dve"]
    LNENG = ["dve"]
    XTENG = ["dve"]

    def evac(dst, src, reads, writes, eng=None):
        eng = eng or CP[0]
        if eng == "act":
            em.op("act", lambda e: e.copy(dst, src), reads=reads, writes=writes)
        else:
            em.op("dve", lambda e: e.tensor_copy(dst, src), reads=reads, writes=writes)

    def run(g):
        try:
            while True:
                next(g)
        except StopIteration as stop:
            return stop.value

    def pipeline(gens, depth):
        it = iter(gens)
        live = []
        more = True
        while True:
            if more and len(live) < depth:
                g = next(it, None)
                if g is None:
                    more = False
                else:
                    live.append(g)
            if not live:
                break
            for g in list(live):
                try:
                    next(g)
                except StopIteration:
                    live.remove(g)

    def weave_spread(main, n_main, others, n_others):
        others = [g for g in others if g is not None]
        k = max(1, n_main // max(1, n_others))
        oi = 0
        done = False
        while not done or others:
            for _ in range(k):
                if not done:
                    try:
                        next(main)
                    except StopIteration:
                        done = True
            if others:
                g = others[oi % len(others)]
                try:
                    next(g)
                    oi += 1
                except StopIteration:
                    others.remove(g)

    def delayed(g, n):
        if g is None:
            return
        for _ in range(n):
            yield
        yield from g

    def weave(*gens):
        live = [g for g in gens if g is not None]
        while live:
            for g in list(live):
                try:
                    next(g)
                except StopIteration:
                    live.remove(g)

    def transposes(src_fn, n, T, dst_flat, dstres, srcres, eng="dve"):
        if eng == "cp":
            eng = CP[0]
        pb, pres = next_P()
        pT = pb.bitcast(BF16)
        for j in range(n):
            em.op("pe", lambda e, j=j: e.transpose(pT[:, j * T:(j + 1) * T], src_fn(j), ident[0:T, 0:T]),
                  reads=list(srcres) + ["ident"], writes=[pres])
        src3 = pT[:, 0:n * T].rearrange("p (a b) -> p a b", a=n)
        if eng == "act":
            em.op("act", lambda e: e.copy(dst_flat, src3), reads=[pres], writes=list(dstres))
        else:
            em.op("dve", lambda e: e.tensor_copy(dst_flat, src3), reads=[pres], writes=list(dstres))

    def make_xT(xb, xbn, T, rot=None):
        xT, xTn = (rot or xTR).next()
        transposes(lambda j: xb[0:T, j * 128:(j + 1) * 128], 8, T, xT[:, :, 0:T], [xTn], [xbn], eng=XTENG[0])
        return xT, xTn

    def proj(xT, xTn, T, c0, N=512):
        pb, pres = next_P()
        for ch in range(8):
            em.op("pe", lambda e, ch=ch: e.matmul(pb[0:T, 0:N], xT[:, ch, 0:T], WR[:, ch, c0:c0 + N],
                                                   start=(ch == 0), stop=(ch == 7)),
                  reads=[xTn] + WRr(c0, c0 + N), writes=[pres])
        return pb, pres

    def rope(pb, pres, T, dst, dstn, cos, sin, tabres, ncol=512):
        g = ncol // 64
        src = pb[0:T, 0:ncol].rearrange("p (g d) -> p g d", g=g)
        d3 = dst[0:T, 0:ncol].rearrange("p (g d) -> p g d", g=g)
        cb = cos.unsqueeze(1).to_broadcast([T, g, 8])
        sbb = sin.unsqueeze(1).to_broadcast([T, g, 8])
        rp, rpn = ropeR.next()
        tt = [rp[0:T, i, 0:g * 8].rearrange("p (g d) -> p g d", g=g) for i in range(4)]
        tn = [(rpn, i) for i in range(4)]
        evac(dst[0:T, 0:ncol], pb[0:T, 0:ncol], [pres], [dstn])
        em.op("dve", lambda e: e.tensor_tensor(tt[0], src[:, :, 0:8], cb, ALU.mult), reads=[pres] + tabres, writes=[tn[0]])
        em.op("dve", lambda e: e.tensor_tensor(tt[1], src[:, :, 8:16], sbb, ALU.mult), reads=[pres] + tabres, writes=[tn[1]])
        em.op("dve", lambda e: e.tensor_tensor(tt[2], src[:, :, 8:16], cb, ALU.mult), reads=[pres] + tabres, writes=[tn[2]])
        em.op("dve", lambda e: e.tensor_tensor(tt[3], src[:, :, 0:8], sbb, ALU.mult), reads=[pres] + tabres, writes=[tn[3]])
        em.op("dve", lambda e: e.tensor_tensor(d3[:, :, 0:8], tt[0], tt[1], ALU.subtract), reads=[tn[0], tn[1], dstn], writes=[dstn])
        em.op("dve", lambda e: e.tensor_tensor(d3[:, :, 8:16], tt[2], tt[3], ALU.add), reads=[tn[2], tn[3], dstn], writes=[dstn])

    def gate(pb, pres, T):
        f, fn = f512R.next()
        sg, sgn = sgR.next()
        em.op("act", lambda e: e.activation(f[0:T, :], pb[0:T, 0:512], AF.Tanh, scale=0.5), reads=[pres], writes=[fn])
        em.op("dve", lambda e: e.scalar_tensor_tensor(sg[0:T, :], f[0:T, :], 1.0, pb[0:T, 0:512], ALU.add, ALU.mult),
              reads=[fn, pres], writes=[sgn])
        return sg, sgn

    def to_T4(pb, pres, T):
        b, bn = b512R.next()
        evac(b[0:T, :], pb[0:T, 0:512], [pres], [bn])
        return b_to_T4(b, bn, T)

    def b_to_T4(b, bn, T):
        qT, qTn = qTR.next()
        transposes(lambda j: b[0:T, j * 128:(j + 1) * 128], 4, T, qT[:, :, 0:T], [qTn], [bn])
        return qT, qTn

    def rsqrt_small(dst, src_ap, n, T, scale, resr, resw):
        em.op("dve", lambda e: e.tensor_scalar(dst, src_ap, scale, LN_EPS, ALU.mult, ALU.add), reads=resr, writes=resw)
        em.op("pool", lambda e: e.tensor_tensor(dst, dst, cneg[0:T, 0:n], ALU.pow), reads=resw + ["cneg"], writes=resw)

    def diff_attn(QT, QTn, T, nkt, diag, kts=None, acc=None):
        if kts is None:
            kts = list(range(nkt))
        kfirst, klast = kts[0], kts[-1]
        Osb, Osbn = acc if acc is not None else OsbR.next()
        for h in range(4):
            groups = [kts[g:g + 4] for g in range(0, len(kts), 4)]
            SE = []

            def qk(grp):
                S, Sres = next_S()
                for i, kt in enumerate(grp):
                    for c in range(2):
                        em.op("pe", lambda e, S=S, i=i, kt=kt, c=c, h=h: e.matmul(
                            S[:, c * 512 + i * 128: c * 512 + i * 128 + T],
                            KT[c * 64:(c + 1) * 64, h, kt * 128:(kt + 1) * 128],
                            QT[c * 64:(c + 1) * 64, h, 0:T], start=True, stop=True),
                            reads=[("KT", kt), QTn], writes=[Sres[c]])
                return S, Sres

            def pv(E, En, grp):
                for c in range(2):
                    for i, kt in enumerate(grp):
                        em.op("pe", lambda e, E=E, i=i, kt=kt, c=c, h=h: e.matmul(
                            psO[0:T, c * 512: c * 512 + 129], E[:, c, i, 0:T], VA[:, kt, h, :],
                            start=(kt == kfirst), stop=(kt == klast)),
                            reads=[(En, c), ("VA", kt)], writes=[ORES[c]])

            nxt = qk(groups[0])
            pending = None
            for gi, grp in enumerate(groups):
                S, Sres = nxt
                if gi + 1 < len(groups):
                    nxt = qk(groups[gi + 1])
                E, En = ER.next()
                n = len(grp)
                S4 = S[:].rearrange("p (c k q) -> p c k q", c=2, k=4)
                for c in range(2):
                    em.op("act", lambda e, S4=S4, E=E, n=n, c=c: e.activation(
                        E[:, c, 0:n, 0:T], S4[:, c, 0:n, 0:T], AF.Exp, scale=0.125), reads=[Sres[c]], writes=[(En, c)])
                    if diag is not None and diag in grp:
                        i = grp.index(diag)
                        em.op("act", lambda e, E=E, i=i, S4=S4, c=c: e.activation(
                            E[64:128, c, i, 0:64], S4[64:128, c, i, 0:64], AF.Identity, scale=0.0),
                            reads=[Sres[c], (En, c)], writes=[(En, c)])
                yield
                if pending is not None:
                    pv(*pending)
                pending = (E, En, grp)
            pv(*pending)
            if acc is None:
                em.op("dve", lambda e, h=h: e.tensor_copy(Osb[0:T, h, 0, :], psO[0:T, 0:129]), reads=[ORES[0]], writes=[Osbn])
                evac(Osb[0:T, h, 1, :], psO[0:T, 512:641], [ORES[1]], [Osbn])
            else:
                em.op("dve", lambda e, h=h: e.tensor_tensor(Osb[0:T, h, 0, :], Osb[0:T, h, 0, :], psO[0:T, 0:129], ALU.add),
                      reads=[ORES[0], Osbn], writes=[Osbn])
                em.op("dve", lambda e, h=h: e.tensor_tensor(Osb[0:T, h, 1, :], Osb[0:T, h, 1, :], psO[0:T, 512:641], ALU.add),
                      reads=[ORES[1], Osbn], writes=[Osbn])
        return Osb, Osbn

    def diff_post(Osb, Osbn, T, sg, sgn, mix, mixn):
        sm, smn = smR.next()
        w1, w1n = w512R.next()
        w2, w2n = w512R.next()
        rz = sm[0:T, 0:8].rearrange("p (h c) -> p h c", c=2)
        em.op("dve", lambda e: e.tensor_scalar(rz, Osb[0:T, :, :, 128], 1e-30, None, ALU.add), reads=[Osbn], writes=[smn])
        em.op("dve", lambda e: e.reciprocal(rz, rz), reads=[smn], writes=[smn])
        em.op("dve", lambda e: e.tensor_scalar(sm[0:T, 8:12], rz[:, :, 1], neg_lam[0:T, :], None, ALU.mult), reads=[smn, "lams"], writes=[smn])
        w1v = w1[0:T, :].rearrange("p (h d) -> p h d", h=4)
        w2v = w2[0:T, :].rearrange("p (h d) -> p h d", h=4)
        em.op("dve", lambda e: e.tensor_tensor(w1v, Osb[0:T, :, 0, 0:128], rz[:, :, 0:1].to_broadcast([T, 4, 128]), ALU.mult), reads=[Osbn, smn], writes=[w1n])
        em.op("dve", lambda e: e.tensor_tensor(w2v, Osb[0:T, :, 1, 0:128], sm[0:T, 8:12].unsqueeze(2).to_broadcast([T, 4, 128]), ALU.mult), reads=[Osbn, smn], writes=[w2n])
        em.op("dve", lambda e: e.tensor_tensor(w1[0:T, :], w1[0:T, :], w2[0:T, :], ALU.add), reads=[w1n, w2n], writes=[w1n])
        em.op("dve", lambda e: e.tensor_tensor(w2[0:T, :], w1[0:T, :], w1[0:T, :], ALU.mult), reads=[w1n, w2n], writes=[w2n])
        em.op("dve", lambda e: e.reduce_sum(sm[0:T, 12:16], w2v, axis=AX.X), reads=[w2n, smn], writes=[smn])
        rsqrt_small(sm[0:T, 16:20], sm[0:T, 12:16], 4, T, 1.0 / 128, [smn], [smn])
        em.op("dve", lambda e: e.tensor_tensor(w1v, w1v, sm[0:T, 16:20].unsqueeze(2).to_broadcast([T, 4, 128]), ALU.mult), reads=[w1n, smn], writes=[w1n])
        em.op("dve", lambda e: e.tensor_tensor(w1v, w1v, gtab[0:T, :].unsqueeze(1).to_broadcast([T, 4, 128]), ALU.mult), reads=[w1n, "gtab"], writes=[w1n])
        em.op(MIXENG[0], lambda e: e.tensor_tensor(mix[0:T, 0:512], w1[0:T, :], sg[0:T, :], ALU.mult), reads=[w1n, sgn], writes=[mixn])

    def mem_pe(mqT, mqTn, T, mk, mkn, mv, mvn):
        S, Sres = next_S()
        for h in range(4):
            for kt in range(2):
                i = h * 2 + kt
                em.op("pe", lambda e, h=h, kt=kt, i=i: e.matmul(
                    S[:, i * 128: i * 128 + T], mk[:, h, kt * 128:(kt + 1) * 128], mqT[:, h, 0:T],
                    start=True, stop=True), reads=[mkn, mqTn], writes=[Sres[i // 4]])
        E, En = ER.next()
        Ef = E[:].rearrange("p a b c -> p (a b) c")
        em.op("act", lambda e: e.activation(Ef[:, :, 0:T], S[:].rearrange("p (k q) -> p k q", k=8)[:, :, 0:T],
                                            AF.Exp, scale=128.0 ** -0.5), reads=list(Sres), writes=[(En, 0), (En, 1)])
        yield
        for h in range(4):
            for kt in range(2):
                em.op("pe", lambda e, h=h, kt=kt: e.matmul(
                    psO[0:T, (h // 2) * 512 + (h % 2) * 129: (h // 2) * 512 + (h % 2) * 129 + 129],
                    Ef[:, h * 2 + kt, 0:T], mv[:, kt, h, :], start=(kt == 0), stop=(kt == 1)),
                    reads=[(En, 0), (En, 1), mvn], writes=[ORES[h // 2]])
        Om, Omn = OmsbR.next()
        em.op("dve", lambda e: e.tensor_copy(Om[0:T, 0:2, :], psO[0:T, 0:258].rearrange("p (h d) -> p h d", h=2)), reads=[ORES[0]], writes=[Omn])
        evac(Om[0:T, 2:4, :], psO[0:T, 512:770].rearrange("p (h d) -> p h d", h=2), [ORES[1]], [Omn])
        yield
        return Om, Omn

    def mem_post(Om, Omn, T, sg, sgn, mix, mixn):
        sm, smn = smR.next()
        w1, w1n = w512R.next()
        em.op("dve", lambda e: e.reciprocal(sm[0:T, 0:4], Om[0:T, :, 128]), reads=[Omn], writes=[smn])
        em.op("dve", lambda e: e.tensor_scalar(sm[0:T, 0:4], sm[0:T, 0:4], 0.5, None, ALU.mult), reads=[smn], writes=[smn])
        w1v = w1[0:T, :].rearrange("p (h d) -> p h d", h=4)
        em.op("dve", lambda e: e.tensor_tensor(w1v, Om[0:T, :, 0:128], sm[0:T, 0:4].unsqueeze(2).to_broadcast([T, 4, 128]), ALU.mult), reads=[Omn, smn], writes=[w1n])
        em.op(MIXENG[0], lambda e: e.tensor_tensor(mix[0:T, 512:1024], w1[0:T, :], sg[0:T, :], ALU.mult), reads=[w1n, sgn], writes=[mixn])

    def swa_pe(QsT, QsTn, T, ktiles, masks):
        EE = []
        for kv in range(2):
            S, Sres = next_S()
            for pl in range(2):
                p = kv * 2 + pl
                for kt in range(2):
                    K2, K2n, Vs, Vsn, nk = ktiles[kt]
                    for ee in range(2):
                        em.op("pe", lambda e, pl=pl, kt=kt, ee=ee, K2=K2, nk=nk, p=p, S=S, kv=kv: e.matmul(
                            S[0:nk, ee * 512 + (pl * 2 + kt) * 128: ee * 512 + (pl * 2 + kt) * 128 + T],
                            K2[ee * 64:(ee + 1) * 64, kv, 0:nk], QsT[ee * 64:(ee + 1) * 64, p, 0:T],
                            start=True, stop=True), reads=[K2n, QsTn], writes=[Sres[ee]])
            E, En = ER.next()
            S5 = S[:].rearrange("p (c a k q) -> p c a k q", c=2, a=2, k=2)
            E5 = E[:].rearrange("p c (a k) q -> p c a k q", a=2)
            for kt in range(2):
                nk = ktiles[kt][4]
                em.op("act", lambda e, kt=kt, nk=nk, S5=S5, E5=E5: e.activation(
                    E5[0:nk, :, :, kt, 0:T], S5[0:nk, :, :, kt, 0:T], AF.Exp, scale=0.125), reads=list(Sres), writes=[(En, 0), (En, 1)])
            if masks:
                em.op("act", lambda e, E5=E5, S5=S5: e.activation(E5[0:64, :, :, 0, 64:128], S5[0:64, :, :, 0, 64:128], AF.Identity, scale=0.0),
                      reads=list(Sres) + [(En, 0), (En, 1)], writes=[(En, 0), (En, 1)])
                em.op("act", lambda e, E5=E5, S5=S5: e.activation(E5[64:128, :, :, 1, 0:64], S5[64:128, :, :, 1, 0:64], AF.Identity, scale=0.0),
                      reads=list(Sres) + [(En, 0), (En, 1)], writes=[(En, 0), (En, 1)])
            EE.append((E, En))
            yield
        for kv in range(2):
            E, En = EE[kv]
            for pl in range(2):
                for ee in range(2):
                    hd = kv * 4 + pl * 2 + ee
                    for kt in range(2):
                        K2, K2n, Vs, Vsn, nk = ktiles[kt]
                        em.op("pe", lambda e, pl=pl, ee=ee, kt=kt, hd=hd, Vs=Vs, nk=nk, E=E, kv=kv: e.matmul(
                            psO[0:T, (hd // 4) * 512 + (hd % 4) * 65: (hd // 4) * 512 + (hd % 4) * 65 + 65],
                            E[0:nk, ee, pl * 2 + kt, 0:T], Vs[0:nk, kv, :], start=(kt == 0), stop=(kt == 1)),
                            reads=[(En, 0), (En, 1), Vsn], writes=[ORES[hd // 4]])
            if kv == 0:
                yield
        Osb_, Osn = OsbR.next()
        Os = Osb_[:].rearrange("p a b c -> p (a b c)")[:, 0:520].rearrange("p (h d) -> p h d", h=8)
        em.op("dve", lambda e: e.tensor_copy(Os[0:T, 0:4, :], psO[0:T, 0:260].rearrange("p (h d) -> p h d", h=4)), reads=[ORES[0]], writes=[Osn])
        evac(Os[0:T, 4:8, :], psO[0:T, 512:772].rearrange("p (h d) -> p h d", h=4), [ORES[1]], [Osn])
        yield
        return Os, Osn

    def swa_post(Os, Osn, T, sg, sgn, mix, mixn):
        sm, smn = smR.next()
        w1, w1n = w512R.next()
        em.op("dve", lambda e: e.tensor_tensor(sm[0:T, 0:8], Os[0:T, :, 64], esink[0:T, :], ALU.add), reads=[Osn, "esink"], writes=[smn])
        em.op("dve", lambda e: e.reciprocal(sm[0:T, 8:16], sm[0:T, 0:8]), reads=[smn], writes=[smn])
        em.op("dve", lambda e: e.tensor_scalar(sm[0:T, 8:16], sm[0:T, 8:16], 0.5, None, ALU.mult), reads=[smn], writes=[smn])
        w1v = w1[0:T, :].rearrange("p (h d) -> p h d", h=8)
        em.op("dve", lambda e: e.tensor_tensor(w1v, Os[0:T, :, 0:64], sm[0:T, 8:16].unsqueeze(2).to_broadcast([T, 8, 64]), ALU.mult), reads=[Osn, smn], writes=[w1n])
        em.op(MIXENG[0], lambda e: e.tensor_tensor(mix[0:T, 0:512], w1[0:T, :], sg[0:T, :], ALU.mult), reads=[w1n, sgn], writes=[mixn])

    def merge_ln(mix, mixn, T, wc0, hb, hbn, l, dst, dstn, rot=None):
        mT, mTn = (rot or mixTR).next()
        transposes(lambda j: mix[0:T, j * 128:(j + 1) * 128], 8, T, mT[:, :, 0:T], [mTn], [mixn], eng="cp")
        yield
        for blk in range(2):
            pb, pres = next_P()
            for ch in range(8):
                em.op("pe", lambda e, blk=blk, ch=ch, pb=pb: e.matmul(
                    pb[0:T, 0:512], mT[:, ch, 0:T], WR[:, ch, wc0 + blk * 512: wc0 + (blk + 1) * 512],
                    start=(ch == 0), stop=(ch == 7)), reads=[mTn] + WRr(wc0 + blk * 512, wc0 + (blk + 1) * 512), writes=[pres])
            em.op("dve", lambda e, blk=blk, pb=pb: e.scalar_tensor_tensor(
                hb[0:T, blk * 512:(blk + 1) * 512], hb[0:T, blk * 512:(blk + 1) * 512], ALPHA,
                pb[0:T, 0:512], ALU.mult, ALU.add), reads=[hbn, pres], writes=[hbn])
            yield
        sm, smn = smR.next()
        for blk in range(2):
            em.op("dve", lambda e, blk=blk: e.bn_stats(sm[0:T, blk * 6:(blk + 1) * 6], hb[0:T, blk * 512:(blk + 1) * 512]), reads=[hbn, smn], writes=[smn])
        em.op("dve", lambda e: e.bn_aggr(sm[0:T, 12:14], sm[0:T, 0:12].rearrange("p (a b) -> p a b", a=2)), reads=[smn], writes=[smn])
        rsqrt_small(sm[0:T, 14:15], sm[0:T, 13:14], 1, T, 1.0, [smn], [smn])
        em.op("dve", lambda e: e.scalar_tensor_tensor(sm[0:T, 15:16], sm[0:T, 12:13], -1.0, sm[0:T, 14:15], ALU.mult, ALU.mult), reads=[smn], writes=[smn])
        if CP[0] == "act":
            em.op("act", lambda e: e.activation(hb[0:T, :], hb[0:T, :], AF.Identity, bias=sm[0:T, 15:16], scale=sm[0:T, 14:15]), reads=[hbn, smn], writes=[hbn])
        else:
            em.op("dve", lambda e: e.tensor_scalar(hb[0:T, :], hb[0:T, :], sm[0:T, 14:15], sm[0:T, 15:16], ALU.mult, ALU.add), reads=[hbn, smn], writes=[hbn])
        em.op(LNENG[0], lambda e: e.tensor_tensor(hb[0:T, :], hb[0:T, :], lngt[l][0:T, :], ALU.mult), reads=[hbn, "lng"], writes=[hbn])
        em.op(LNENG[0], lambda e: e.tensor_tensor(dst[0:T, :], hb[0:T, :], lnbt[l][0:T, :], ALU.add), reads=[hbn, "lnb"], writes=[dstn])

    def load_w(c0, src, ncols):
        for cc in range(0, ncols, 512):
            n = min(512, ncols - cc)
            em.dma(WR[:, :, c0 + cc: c0 + cc + n], src[:, cc:cc + n].rearrange("(ch p) n -> p ch n", p=128),
                   writes=WRr(c0 + cc, c0 + cc + n), eng="pool")

    def cached_mem_kv(ksrc, vsrc, mk, mkn, mv, mvn):
        for r in range(2):
            cm_b, cm_bn = b512R.next()
            em.dma(cm_b[:], ksrc[r * 128:(r + 1) * 128, :], writes=[cm_bn], eng="pool")
            pb2, pres2 = next_P()
            pT = pb2.bitcast(BF16)
            for h in range(4):
                em.op("pe", lambda e, h=h, pT=pT, cm_b=cm_b: e.transpose(pT[:, h * 128:(h + 1) * 128], cm_b[:, h * 128:(h + 1) * 128], ident[:]),
                      reads=[cm_bn, "ident"], writes=[pres2])
            em.op("dve", lambda e, r=r, pT=pT: e.tensor_copy(mk[:, :, r * 128:(r + 1) * 128], pT[:, 0:512].rearrange("p (a b) -> p a b", a=4)),
                  reads=[pres2], writes=[mkn])
            em.dma(mv[:, r, :, 0:128], vsrc[r * 128:(r + 1) * 128, :].rearrange("p (h d) -> p h d", h=4), writes=[mvn], eng="pool")
        em.op("pool", lambda e: e.memset(mv[:, :, :, 128:129], 1.0), reads=[mvn], writes=[mvn])

    xbA = Rot.__new__(Rot)
    xbA.bufs = xbR.bufs + [mixR.bufs[0], mixs]
    xbA.names = xbR.names + [mixR.names[0], "mixs"]
    xbA.i = -1
    pend = {}

    def prefetch(t):
        if t <= NT and t not in pend:
            xb, xbn = xbA.next()
            if t < NT:
                em.dma(xb[:, :], xloc[t * 128:(t + 1) * 128, :], writes=[xbn], eng="pool")
            else:
                em.dma(xb[0:TS, :], xs_d, writes=[xbn], eng="pool")
            pend[t] = (xb, xbn)

    MEMR = [("KT", t) for t in range(16)]
    em.dma(memb, memp.rearrange("(r p) n -> p r n", p=128), writes=MEMR, eng="pool")
    load_w(0, w_mem[0], 1024)
    load_w(1024, w_mem[1], 1024)
    load_w(2048, w_in_a[:, 512:1536], 1024)
    for t_ in range(4):
        prefetch(t_)
    for r in range(2):
        pb, pres = next_P()
        pT = pb.bitcast(BF16)
        for j in range(8):
            em.op("pe", lambda e, j=j, r=r, pT=pT: e.transpose(pT[:, j * 128:(j + 1) * 128], memb[:, r, j * 128:(j + 1) * 128], ident[:]),
                  reads=MEMR + ["ident"], writes=[pres])
        em.op("dve", lambda e, r=r, pT=pT: e.tensor_copy(memT[:, :, r * 128:(r + 1) * 128], pT[:, 0:1024].rearrange("p (a b) -> p a b", a=8)),
              reads=[pres], writes=["memT"])
    for l in range(2):
        mkn, mvn = ("mkT0", "mva0") if l == 0 else ("mkTs", "mvas")
        for r in range(2):
            for kvi in range(2):
                c0 = l * 1024 + kvi * 512
                pb, pres = next_P()
                for ch in range(8):
                    em.op("pe", lambda e, ch=ch, r=r, c0=c0, pb=pb: e.matmul(
                        pb[:, 0:512], memT[:, ch, r * 128:(r + 1) * 128], WR[:, ch, c0:c0 + 512],
                        start=(ch == 0), stop=(ch == 7)), reads=["memT"] + WRr(c0, c0 + 512), writes=[pres])
                f, fn = f512R.next()
                em.op("act", lambda e, f=f, pb=pb: e.copy(f[:], pb[:, 0:512]), reads=[pres], writes=[fn])
                em.dma((mk_p if kvi == 0 else mv_p)[l, r * 128:(r + 1) * 128, :], f[:], reads=[fn])
                if kvi == 0:
                    b, bn = b512R.next()
                    em.op("dve", lambda e, b=b, pb=pb: e.tensor_copy(b[:], pb[:, 0:512]), reads=[pres], writes=[bn])
                    pb2, pres2 = next_P()
                    pT = pb2.bitcast(BF16)
                    for h in range(4):
                        em.op("pe", lambda e, h=h, b=b, pT=pT: e.transpose(pT[:, h * 128:(h + 1) * 128], b[:, h * 128:(h + 1) * 128], ident[:]),
                              reads=[bn, "ident"], writes=[pres2])
                    em.op("dve", lambda e, l=l, r=r, pT=pT: e.tensor_copy(mkT[l][:, :, r * 128:(r + 1) * 128], pT[:, 0:512].rearrange("p (a b) -> p a b", a=4)),
                          reads=[pres2], writes=[mkn])
                else:
                    em.op("dve", lambda e, l=l, r=r, pb=pb: e.tensor_copy(mva[l][:, r, :, 0:128], pb[:, 0:512].rearrange("p (h d) -> p h d", h=4)),
                          reads=[pres], writes=[mvn])
                    em.op("pool", lambda e, l=l, r=r: e.memset(mva[l][:, r, :, 128:129], 1.0), reads=[mvn], writes=[mvn])
    em.dma(mkT1_scr, mkTs[:].rearrange("p a b -> p (a b)"), reads=["mkTs"], writes=["mkT1_scr"])
    em.dma(mva1_scr, mvas[:].rearrange("p a b c -> p (a b c)"), reads=["mvas"], writes=["mva1_scr"])

    def late_weights():
        load_w(0, w_in_a[:, 0:512], 512)
        load_w(512, w_in_a[:, 1536:3072], 1536)

    def load_xb(row0, T, src):
        xb, xbn = xbR.next()
        em.dma(xb[0:T, :], src[row0:row0 + T, :], writes=[xbn], eng="pool")
        return xb, xbn

    def kv_gen(xb, xbn, T, t, cos, sin, tabres, kdst, vdst, ones_src, extra_w=()):
        xT, xTn = make_xT(xb, xbn, T)
        yield
        pb, pres = proj(xT, xTn, T, 2048)
        kf, kfn = sgR.next()
        rope(pb, pres, T, kf, kfn, cos, sin, tabres)
        if kdst is not None:
            em.dma(kdst, kf[0:T, :], reads=[kfn], writes=["dks_d"] if T == TS else [])
        kb4, kbn = qTR.next()
        kb = kb4[:].rearrange("p a b -> p (a b)")
        em.op("act", lambda e: e.copy(kb[0:T, :], kf[0:T, :]), reads=[kfn], writes=[kbn])
        yield
        pb3, pres3 = proj(xT, xTn, T, 2560)
        if vdst is not None:
            vf, vfn = sgR.next()
            em.op("act", lambda e: e.copy(vf[0:T, :], pb3[0:T, 0:512]), reads=[pres3], writes=[vfn])
            em.dma(vdst, vf[0:T, :], reads=[vfn], writes=["dvs_d"] if T == TS else [])
        em.op("dve", lambda e: e.tensor_copy(VA[0:T, t, :, 0:128], pb3[0:T, 0:512].rearrange("p (h d) -> p h d", h=4)),
              reads=[pres3], writes=[("VA", t)])
        if ones_src is None:
            em.op("pool", lambda e: e.memset(VA[0:T, t, :, 128:129], 1.0), reads=[("VA", t)], writes=[("VA", t)])
        else:
            em.op("pool", lambda e: e.tensor_copy(VA[0:T, t, :, 128:129], ones_src), reads=[("VA", t), "pflag"], writes=[("VA", t)])
        yield
        pb2, pres2 = next_P()
        pT = pb2.bitcast(BF16)
        for h in range(4):
            em.op("pe", lambda e, h=h: e.transpose(pT[:, h * T:(h + 1) * T], kb[0:T, h * 128:(h + 1) * 128], ident[0:T, 0:T]),
                  reads=[kbn, "ident"], writes=[pres2])
        em.op("dve", lambda e: e.tensor_copy(KT[:, :, t * 128: t * 128 + T], pT[:, 0:4 * T].rearrange("p (a b) -> p a b", a=4)),
              reads=[pres2], writes=[("KT", t)] + list(extra_w))
        yield

    def kv_all():
        for t in range(NT):
            own = t >= OWN0
            for a in range(4):
                prefetch(t + a)
            if t == 6:
                late_weights()
            xb_, xbn_ = pend.pop(t)
            yield kv_gen(xb_, xbn_, 128, t, cosP[:, t, :], sinP[:, t, :], ["cosP", "sinP"],
                         dk_p[(t - OWN0) * 128:(t - OWN0 + 1) * 128, :] if own else None,
                         dv_p[(t - OWN0) * 128:(t - OWN0 + 1) * 128, :] if own else None,
                         None if own else pflag[:, 0:1].unsqueeze(1).to_broadcast([128, 4, 1]),
                         extra_w=["memT"] + MEMR if t == 0 else ())
        xb_, xbn_ = pend.pop(NT)
        yield kv_gen(xb_, xbn_, TS, NT, cosS[0:TS, :], sinS[0:TS, :], ["cosS", "sinS"], dk_s, dv_s, None)

    pipeline(kv_all(), 5)
    P_POOL[0] = P_SMALL

    def load_hb(src, row0, T):
        hb, hbn = hR.next()
        em.dma(hb[0:T, :], src[row0:row0 + T, :], writes=[hbn])
        return hb, hbn

    def layer_front(xT, xTn, T, cos, sin, tabres):
        pb, pres = proj(xT, xTn, T, 0)
        qb, qbn = b512R.next()
        rope(pb, pres, T, qb, qbn, cos, sin, tabres)
        yield
        pb, pres = proj(xT, xTn, T, 512)
        sg1, sg1n = gate(pb, pres, T)
        yield
        QT, QTn = b_to_T4(qb, qbn, T)
        pb, pres = proj(xT, xTn, T, 1024)
        mb, mbn = b512R.next()
        evac(mb[0:T, :], pb[0:T, 0:512], [pres], [mbn])
        yield
        pb, pres = proj(xT, xTn, T, 1536)
        sg2, sg2n = gate(pb, pres, T)
        yield
        mqT, mqTn = b_to_T4(mb, mbn, T)
        yield
        return QT, QTn, sg1, sg1n, mqT, mqTn, sg2, sg2n

    ctx = {}

    xbs = {}
    hbs = {}

    def front(t):
        xb, xbn = xbs.pop(t)
        xT, xTn = make_xT(xb, xbn, 128)
        yield
        ctx[t] = yield from layer_front(xT, xTn, 128, cosP[:, t, :], sinP[:, t, :], ["cosP", "sinP"])

    def attn(t):
        QT, QTn, sgd, sgdn, mqT, mqTn, sgm, sgmn = ctx[t]
        Osb, Osbn = yield from diff_attn(QT, QTn, 128, t + 1, t)
        Om, Omn = yield from mem_pe(mqT, mqTn, 128, mkT[0], "mkT0", mva[0], "mva0")
        mix, mixn = mixPR.next()
        diff_post(Osb, Osbn, 128, sgd, sgdn, mix, mixn)
        yield
        mem_post(Om, Omn, 128, sgm, sgmn, mix, mixn)
        ctx[t] = (mix, mixn)
        yield

    def back(t):
        mix, mixn = ctx.pop(t)
        hb, hbn = hbs.pop(t)
        yield from merge_ln(mix, mixn, 128, 3072, hb, hbn, 0, hb, hbn)
        em.dma(scr[(t - T0) * 128:(t - T0 + 1) * 128, :], hb[:], reads=[hbn], writes=[("scr", t)])
        yield

    load_w(3072, w_out[0], 1024)
    load_w(2048, w_out[1], 1024)
    CP[0] = "dve"
    XTENG[0] = "act"
    xbs[T0] = load_xb(T0 * 128, 128, xloc)
    xbs[T0 + 1] = load_xb((T0 + 1) * 128, 128, xloc)
    run(front(T0))
    for t in range(T0, NT):
        if t + 2 < NT:
            xbs[t + 2] = load_xb((t + 2) * 128, 128, xloc)
        hbs[t] = load_hb(xloc, t * 128, 128)
        n_main = 4 * ((t + 1 + 3) // 4) + 4
        weave_spread(attn(t), n_main, [delayed(back(t - 1), 3) if t > T0 else None, front(t + 1) if t + 1 < NT else None], 14)
    run(back(NT - 1))

    xsb, xsbn = load_xb(0, TS, xs_d)
    xsT, xsTn = make_xT(xsb, xsbn, TS)
    QTs, QTsn, sgd_s, sgd_sn, mqTs, mqTsn, sgm_s, sgm_sn = run(layer_front(xsT, xsTn, TS, cosS[0:TS, :], sinS[0:TS, :], ["cosS", "sinS"]))
    load_w(0, w_in_b, 2048)
    xs1b = xs1.bitcast(BF16)
    vstage = [(xs1b[:, 0:1024].rearrange("p (r n) -> p r n", r=2), "vst0"),
              (xs1b[:, 1024:2048].rearrange("p (r n) -> p r n", r=2), "vst1")]
    vsi = [0]

    def fill(bb, half):
        for k4 in range(half * 4, half * 4 + 4):
            stg, stgn = hR.next()
            stgb = stg.bitcast(BF16)
            em.dma(stgb[:, :].rearrange("p (r n) -> p r n", r=4),
                   cdk[bb, k4 * 512:(k4 + 1) * 512, :].rearrange("(r p) n -> p r n", p=128), writes=[stgn], eng="pool")
            em.op("pool", lambda e, k4=k4: e.memset(VA[:, k4 * 4:(k4 + 1) * 4, :, 128:129], 1.0),
                  writes=[("VA", k4 * 4 + r) for r in range(4)])
            for pr in range(2):
                vs, vsn = vstage[vsi[0]]
                vsi[0] ^= 1
                t0_ = k4 * 4 + pr * 2
                em.dma(vs, cdv[bb, t0_ * 128:(t0_ + 2) * 128, :].rearrange("(r p) n -> p r n", p=128), writes=[vsn], eng="pool")
                for r2 in range(2):
                    kt2 = t0_ + r2
                    if r2:
                        em.op("dve", lambda e, vs=vs, r2=r2, kt2=kt2: e.tensor_copy(
                            VA[:, kt2, :, 0:128], vs[:, r2, :].rearrange("p (h d) -> p h d", h=4)),
                            reads=[vsn, ("VA", kt2)], writes=[("VA", kt2)])
                    else:
                        em.op("act", lambda e, vs=vs, r2=r2, kt2=kt2: e.copy(
                            VA[:, kt2, :, 0:128], vs[:, r2, :].rearrange("p (h d) -> p h d", h=4)),
                            reads=[vsn, ("VA", kt2)], writes=[("VA", kt2)])
            for r in range(4):
                kt = k4 * 4 + r
                pb2, pres2 = next_P()
                pT = pb2.bitcast(BF16)
                for h in range(4):
                    em.op("pe", lambda e, h=h, r=r, stgb=stgb, pT=pT: e.transpose(
                        pT[:, h * 128:(h + 1) * 128], stgb[:, r * 512 + h * 128: r * 512 + (h + 1) * 128], ident[:]),
                        reads=[stgn, "ident"], writes=[pres2])
                if kt % 2:
                    em.op("dve", lambda e, kt=kt, pT=pT: e.tensor_copy(KT[:, :, kt * 128:(kt + 1) * 128], pT[:, 0:512].rearrange("p (a b) -> p a b", a=4)),
                          reads=[pres2], writes=[("KT", kt)])
                else:
                    em.op("act", lambda e, kt=kt, pT=pT: e.copy(KT[:, :, kt * 128:(kt + 1) * 128], pT[:, 0:512].rearrange("p (a b) -> p a b", a=4)),
                          reads=[pres2], writes=[("KT", kt)])
                if r % 2:
                    yield
        if half == 0:
            return
        r0 = bb * 32
        kb, kbn = b512R.next()
        em.dma(kb[0:TS, :], dk_s, reads=["dks_d"], writes=[kbn], eng="pool")
        pb2, pres2 = next_P()
        pT = pb2.bitcast(BF16)
        for h in range(4):
            em.op("pe", lambda e, h=h, kb=kb, pT=pT: e.transpose(pT[:, h * TS:(h + 1) * TS], kb[0:TS, h * 128:(h + 1) * 128], ident[0:TS, 0:TS]),
                  reads=[kbn, "ident"], writes=[pres2])
        em.op("pool", lambda e: e.memset(KT[:, :, NT * 128:(NT + 1) * 128], 0.0), writes=[("KT", NT)])
        em.op("dve", lambda e, pT=pT: e.tensor_copy(KT[:, :, NT * 128: NT * 128 + TS], pT[:, 0:4 * TS].rearrange("p (a b) -> p a b", a=4)),
              reads=[pres2, ("KT", NT)], writes=[("KT", NT)])
        em.op("pool", lambda e: e.memset(VA[:, NT, :, :], 0.0), writes=[("VA", NT)])
        em.dma(VA[r0:r0 + 16, NT, :, 0:128], dv_s[r0:r0 + 16, :].rearrange("p (h d) -> p h d", h=4), reads=["dvs_d", ("VA", NT)], writes=[("VA", NT)], eng="pool")
        em.op("pool", lambda e, r0=r0: e.memset(VA[r0:r0 + 16, NT, :, 128:129], 1.0), reads=[("VA", NT)], writes=[("VA", NT)])
        yield

    osb0 = {}
    H0 = list(range(0, 16))
    H1 = list(range(16, NT + 1))

    def sattn(bb, half):
        if half == 0:
            osb0[bb] = yield from diff_attn(QTs, QTsn, TS, NT + 1, None, kts=H0)
            return
        Osb, Osbn = yield from diff_attn(QTs, QTsn, TS, NT + 1, None, kts=H1, acc=osb0[bb])
        r0 = bb * 32
        mixt, mixtn = mixR.next()
        diff_post(Osb, Osbn, TS, sgd_s, sgd_sn, mixt, mixtn)
        yield
        cached_mem_kv(cmk[0, bb], cmv[0, bb], mkTs, "mkTs", mvas, "mvas")
        yield
        Om, Omn = yield from mem_pe(mqTs, mqTsn, TS, mkTs, "mkTs", mvas, "mvas")
        mem_post(Om, Omn, TS, sgm_s, sgm_sn, mixt, mixtn)
        em.op("dve", lambda e: e.tensor_copy(mixs[r0:r0 + 32, :], mixt[r0:r0 + 32, :]), reads=[mixtn], writes=["mixs"])
        yield

    run(fill(0, 0))
    weave(fill(0, 1), sattn(0, 0))
    weave(fill(1, 0), sattn(0, 1))
    weave(fill(1, 1), sattn(1, 0))
    em.dma(xs1[0:TS, :], xs_d, writes=["xs1", "vst0", "vst1"])
    run(sattn(1, 1))
    run(merge_ln(mixs, "mixs", TS, 3072, xs1, "xs1", 0, xs1, "xs1"))

    load_w(3072, w_kv, 256)
    load_ln(1)
    em.op("dve", lambda e: e.memset(cneg[:, 4:5], -0.5),
          writes=[("KT", t) for t in range(NT + 1)] + L1V_NAMES)

    def shared_kv(xT, xTn, T, cos, sin, tabres, kout, vout, vmask):
        pb, pres = proj(xT, xTn, T, 3072, N=256)
        kf, kfn = f512R.next()
        rope(pb, pres, T, kf, kfn, cos, sin, tabres, ncol=128)
        em.op("act", lambda e: e.copy(kf[0:T, 128:256], pb[0:T, 128:256]), reads=[pres, kfn], writes=[kfn])
        if kout is not None:
            em.dma(kout, kf[0:T, 0:128], reads=[kfn])
            em.dma(vout, kf[0:T, 128:256], reads=[kfn])
        kd, kdn = kdupR.next()
        kdv = kd[0:T, :, :].rearrange("p k (u d) -> p k u d", u=2)
        for u in range(2):
            em.op("pool", lambda e, u=u: e.tensor_copy(kdv[:, :, u, :], kf[0:T, 0:128].rearrange("p (k d) -> p k d", k=2)),
                  reads=[kfn], writes=[kdn])
        Vs, Vsn = VsR.next()
        if vmask is None:
            em.op("dve", lambda e: e.tensor_copy(Vs[0:T, :, 0:64], kf[0:T, 128:256].rearrange("p (k d) -> p k d", k=2)), reads=[kfn], writes=[Vsn])
            em.op("pool", lambda e: e.memset(Vs[0:T, :, 64:65], 1.0), reads=[Vsn], writes=[Vsn])
        else:
            em.op("dve", lambda e: e.tensor_scalar(Vs[0:T, :, 0:64], kf[0:T, 128:256].rearrange("p (k d) -> p k d", k=2), vmask, None, ALU.mult),
                  reads=[kfn, "pflag"], writes=[Vsn])
            em.op("pool", lambda e: e.tensor_copy(Vs[0:T, :, 64:65], vmask.unsqueeze(1).to_broadcast([T, 2, 1])), reads=[Vsn, "pflag"], writes=[Vsn])
        yield
        K2, K2n = K2R.next()
        pb2, pres2 = next_P()
        pT = pb2.bitcast(BF16)
        for kvh in range(2):
            em.op("pe", lambda e, kvh=kvh: e.transpose(pT[:, kvh * T:(kvh + 1) * T], kd[0:T, kvh, :], ident[0:T, 0:T]),
                  reads=[kdn, "ident"], writes=[pres2])
        em.op("dve", lambda e: e.tensor_copy(K2[:, :, 0:T], pT[:, 0:2 * T].rearrange("p (a b) -> p a b", a=2)), reads=[pres2], writes=[K2n])
        yield
        return (K2, K2n, Vs, Vsn, T), kf, kfn

    xsb, xsbn = xbR.next()
    em.op("pool", lambda e: e.tensor_copy(xsb[0:TS, :], xs1[0:TS, :]), reads=["xs1"], writes=[xsbn])
    xsT, xsTn = make_xT(xsb, xsbn, TS)
    newkv, kf, kfn = run(shared_kv(xsT, xsTn, TS, cosS[0:TS, :], sinS[0:TS, :], ["cosS", "sinS"], None, None, None))
    K2n_, K2nn, _, _, _ = newkv
    for bb in range(2):
        em.dma(swk_s[bb, 112:128, :], kf[bb * 32: bb * 32 + 16, 0:128], reads=[kfn])
        em.dma(swv_s[bb, 112:128, :], kf[bb * 32: bb * 32 + 16, 128:256], reads=[kfn])
    for bb in range(2):
        r0 = bb * 32
        em.op("dve", lambda e, bb=bb: e.memset(VmD[bb][0:TS, :, :], 0.0), writes=[f"VmD{bb}"])
        em.op("dve", lambda e, bb=bb, r0=r0: e.tensor_copy(VmD[bb][r0:r0 + 16, :, 0:64], kf[r0:r0 + 16, 128:256].rearrange("p (k d) -> p k d", k=2)),
              reads=[kfn, f"VmD{bb}"], writes=[f"VmD{bb}"])
        em.op("dve", lambda e, bb=bb, r0=r0: e.memset(VmD[bb][r0:r0 + 16, :, 64:65], 1.0), reads=[f"VmD{bb}"], writes=[f"VmD{bb}"])
    QTs, QTsn, sgs_s, sgs_sn, mqTs, mqTsn, sgm_s, sgm_sn = run(layer_front(xsT, xsTn, TS, cosS[0:TS, :], sinS[0:TS, :], ["cosS", "sinS"]))
    for bb in range(2):
        r0 = bb * 32
        em.dma(swk_s[bb, 0:112, :], cswk[bb, 16:128, :])
        em.dma(swv_s[bb, 0:112, :], cswv[bb, 16:128, :])
        em.dma(cswb[:, 0, :], cswk[bb], writes=["cswb"], eng="pool")
        em.dma(cswb[:, 1, :], cswv[bb], writes=["cswb"], eng="pool")
        kd, kdn = kdupR.next()
        kdv = kd[:, :, :].rearrange("p k (u d) -> p k u d", u=2)
        for u in range(2):
            em.op("pool", lambda e, u=u, kdv=kdv: e.tensor_copy(kdv[:, :, u, :], cswb[:, 0, :].rearrange("p (k d) -> p k d", k=2)),
                  reads=["cswb"], writes=[kdn])
        pb2, pres2 = next_P()
        pT = pb2.bitcast(BF16)
        for kvh in range(2):
            em.op("pe", lambda e, kvh=kvh, kd=kd, pT=pT: e.transpose(pT[:, kvh * 128:(kvh + 1) * 128], kd[:, kvh, :], ident[:]),
                  reads=[kdn, "ident"], writes=[pres2])
        em.op("dve", lambda e, pT=pT: e.tensor_copy(K2C[:, :, :], pT[:, 0:256].rearrange("p (a b) -> p a b", a=2)), reads=[pres2], writes=["K2C"])
        Vc, Vcn = VsR.next()
        em.op("dve", lambda e, Vc=Vc: e.tensor_copy(Vc[:, :, 0:64], cswb[:, 1, :].rearrange("p (k d) -> p k d", k=2)), reads=["cswb"], writes=[Vcn])
        em.op("pool", lambda e, Vc=Vc: e.memset(Vc[:, :, 64:65], 1.0), reads=[Vcn], writes=[Vcn])
        Vm, Vmn = VmD[bb], f"VmD{bb}"
        mixt, mixtn = mixR.next()
        Os, Osn = run(swa_pe(QTs, QTsn, TS, [(K2C, "K2C", Vc, Vcn, 128), (K2n_, K2nn, Vm, Vmn, TS)], False))
        swa_post(Os, Osn, TS, sgs_s, sgs_sn, mixt, mixtn)
        cached_mem_kv(cmk[1, bb], cmv[1, bb], mkTs, "mkTs", mvas, "mvas")
        Om, Omn = run(mem_pe(mqTs, mqTsn, TS, mkTs, "mkTs", mvas, "mvas"))
        mem_post(Om, Omn, TS, sgm_s, sgm_sn, mixt, mixtn)
        em.op("dve", lambda e, r0=r0, mixt=mixt: e.tensor_copy(mixs[r0:r0 + 32, :], mixt[r0:r0 + 32, :]), reads=[mixtn], writes=["mixs"])
    run(merge_ln(mixs, "mixs", TS, 2048, xs1, "xs1", 1, xs1, "xs1"))
    em.dma(y_s, xs1[0:TS, :], reads=["xs1"])

    em.dma(mkTs[:].rearrange("p a b -> p (a b)"), mkT1_scr, reads=["mkT1_scr"], writes=["mkTs"])
    em.dma(mvas[:].rearrange("p a b c -> p (a b c)"), mva1_scr, reads=["mva1_scr"], writes=["mvas"])
    ctx2 = {}
    ring = {}
    xts = {}

    xbs2 = {}
    hbs2 = {}

    def load2(t):
        xb, xbn = xbR.next()
        em.dma(xb[:], scr[(t - T0) * 128:(t - T0 + 1) * 128, :], reads=[("scr", t)], writes=[xbn], eng="pool")
        xbs2[t] = (xb, xbn)

    def loadh2(t):
        hb, hbn = hR.next()
        em.dma(hb[:], scr[(t - T0) * 128:(t - T0 + 1) * 128, :], reads=[("scr", t)], writes=[hbn])
        hbs2[t] = (hb, hbn)

    def front2a(t):
        xb, xbn = xbs2.pop(t)
        xT, xTn = make_xT(xb, xbn, 128, rot=xT2R)
        xts[t] = (xT, xTn)
        yield
        last = (t == NT - 1)
        cur, _, _ = yield from shared_kv(xT, xTn, 128, cosP[:, t, :], sinP[:, t, :], ["cosP", "sinP"],
                                         swk_p if last else None, swv_p if last else None,
                                         pflag[:, 0:1] if t == T0 else None)
        ring[t] = cur

    def front2b(t):
        xT, xTn = xts.pop(t)
        ctx2[t] = yield from layer_front(xT, xTn, 128, cosP[:, t, :], sinP[:, t, :], ["cosP", "sinP"])

    def attn2(t):
        QT, QTn, sgs, sgsn, mqT, mqTn, sgm, sgmn = ctx2[t]
        Os, Osn = yield from swa_pe(QT, QTn, 128, [ring[t - 1], ring[t]], True)
        Om, Omn = yield from mem_pe(mqT, mqTn, 128, mkTs, "mkTs", mvas, "mvas")
        mix, mixn = mixPR.next()
        swa_post(Os, Osn, 128, sgs, sgsn, mix, mixn)
        yield
        mem_post(Om, Omn, 128, sgm, sgmn, mix, mixn)
        ctx2[t] = (mix, mixn)
        ring.pop(t - 1)
        yield

    def back2(t):
        mix, mixn = ctx2.pop(t)
        hb, hbn = hbs2.pop(t)
        yield from merge_ln(mix, mixn, 128, 2048, hb, hbn, 1, hb, hbn, rot=mixT2R)
        em.dma(y_p[(t - OWN0) * 128:(t - OWN0 + 1) * 128, :], hb[:], reads=[hbn])
        yield

    CP[0] = "act"
    MIXENG[0] = "pool"
    LNENG[0] = "pool"
    S_SINGLE[0] = True
    P_POOL[0] = P_SMALL + [(psB[:, 0:512], "B0"), (psB[:, 512:1024], "B1")]
    load2(T0)
    load2(OWN0)
    weave(front2a(T0), front2a(OWN0))
    xts.pop(T0)
    load2(OWN0 + 1)
    load2(OWN0 + 2)
    weave(front2a(OWN0 + 1), front2b(OWN0))
    for t in range(OWN0, NT):
        if t + 3 < NT:
            load2(t + 3)
        loadh2(t)
        weave(attn2(t), delayed(back2(t - 1), 2) if t > OWN0 else None,
              front2b(t + 1) if t + 1 < NT else None, front2a(t + 2) if t + 2 < NT else None)
    run(back2(NT - 1))

    stats = em.finalize()
    return nc, stats


_CACHE = {}


def _rope_tab(pos):
    inv = (np.float32(500000.0) ** (-np.arange(8, dtype=np.float32) / np.float32(8))).astype(np.float32)
    ang = pos.astype(np.float32)[:, None] * inv[None, :]
    return np.cos(ang).astype(np.float32), np.sin(ang).astype(np.float32)


def kernel(x_prompt, x_sample, mem_prompt, cache_diff_k, cache_diff_v, cache_swa_k, cache_swa_v,
           cache_mem_k, cache_mem_v, w_in_a, lam_q1, lam_k1, lam_q2, lam_k2, diff_norm_g,
           w_in_b, sinks, w_kv_shared, w_mem_kv, w_out, ln_g, ln_b):
    f = lambda a: np.ascontiguousarray(np.asarray(a), dtype=np.float32)
    x_prompt, x_sample, mem_prompt = f(x_prompt), f(x_sample), f(mem_prompt)
    cache_diff_k, cache_diff_v = f(cache_diff_k), f(cache_diff_v)
    cache_swa_k, cache_swa_v = f(cache_swa_k), f(cache_swa_v)
    cache_mem_k, cache_mem_v = f(cache_mem_k), f(cache_mem_v)
    if "nc" not in _CACHE:
        _CACHE["nc"], _CACHE["stats"] = build_program()
    nc = _CACHE["nc"]
    ident = np.eye(128, dtype=np.float32).astype(ml_dtypes.bfloat16)
    lamv = np.concatenate([f(lam_q1), f(lam_k1), f(lam_q2), f(lam_k2)], axis=0)
    spos = np.full(TS, 4096, dtype=np.int64)
    for bb in range(2):
        spos[bb * 32: bb * 32 + 16] = 4096 + np.arange(16)
    cosS, sinS = _rope_tab(spos)
    in_maps = []
    for c in range(8):
        b, j = c // 2, c % 2
        xl = np.zeros((NT * 128, 1024), np.float32)
        if j == 1:
            xl[:] = x_prompt[b]
        else:
            xl[2048:] = x_prompt[b, 0:2048]
        pos = np.maximum(np.arange(NT * 128) - 2048 * (1 - j), 0)
        cp, sp = _rope_tab(pos)
        cosP = np.ascontiguousarray(cp.reshape(NT, 128, 8).transpose(1, 0, 2))
        sinP = np.ascontiguousarray(sp.reshape(NT, 128, 8).transpose(1, 0, 2))
        xs = np.zeros((TS, 1024), np.float32)
        for bb in range(2):
            xs[bb * 32: bb * 32 + 16] = x_sample[2 * c + bb]
        in_maps.append({
            "xloc": xl, "xs": xs, "memp": mem_prompt[b],
            "cdk": cache_diff_k[0, 2 * c:2 * c + 2].reshape(2, 4096, 512),
            "cdv": cache_diff_v[0, 2 * c:2 * c + 2].reshape(2, 4096, 512),
            "cswk": cache_swa_k[2 * c:2 * c + 2].reshape(2, 128, 128),
            "cswv": cache_swa_v[2 * c:2 * c + 2].reshape(2, 128, 128),
            "cmk": cache_mem_k[:, 2 * c:2 * c + 2].reshape(2, 2, 256, 512),
            "cmv": cache_mem_v[:, 2 * c:2 * c + 2].reshape(2, 2, 256, 512),
            "w_in_a": f(w_in_a)[0], "w_in_b": f(w_in_b)[0], "w_kv": f(w_kv_shared),
            "w_mem": f(w_mem_kv), "w_out": f(w_out), "lamv": lamv, "dng": f(diff_norm_g),
            "sinks": f(sinks), "lng": f(ln_g), "lnb": f(ln_b),
            "cosP": cosP, "sinP": sinP, "cosS": cosS, "sinS": sinS,
            "pflag": np.full((128, 1), float(j), np.float32), "ident": ident,
        })
    res = run_bass_kernel_spmd(nc, in_maps, core_ids=list(range(8)))
    R = [{k: np.asarray(v) for k, v in r.items()} for r in res.results]
    y_prompt = np.zeros((4, 4096, 1024), np.float32)
    dkp = np.zeros((1, 4, 4096, 4, 2, 64), np.float32)
    dvp = np.zeros((1, 4, 4096, 4, 128), np.float32)
    y_sample = np.zeros((16, 16, 1024), np.float32)
    dks = np.zeros((1, 16, 16, 4, 2, 64), np.float32)
    dvs = np.zeros((1, 16, 16, 4, 128), np.float32)
    swkp = np.zeros((4, 128, 2, 64), np.float32)
    swvp = np.zeros((4, 128, 2, 64), np.float32)
    swks = np.zeros((16, 128, 2, 64), np.float32)
    swvs = np.zeros((16, 128, 2, 64), np.float32)
    mkp = np.zeros((2, 4, 256, 4, 128), np.float32)
    mvp = np.zeros((2, 4, 256, 4, 128), np.float32)
    for c in range(8):
        b, j = c // 2, c % 2
        r = R[c]
        sl = slice(j * 2048, (j + 1) * 2048)
        y_prompt[b, sl] = r["y_p"]
        dkp[0, b, sl] = r["dk_p"].reshape(2048, 4, 2, 64)
        dvp[0, b, sl] = r["dv_p"].reshape(2048, 4, 128)
        for bb in range(2):
            rows = slice(bb * 32, bb * 32 + 16)
            y_sample[2 * c + bb] = r["y_s"][rows]
            dks[0, 2 * c + bb] = r["dk_s"][rows].reshape(16, 4, 2, 64)
            dvs[0, 2 * c + bb] = r["dv_s"][rows].reshape(16, 4, 128)
            swks[2 * c + bb] = r["swk_s"][bb].reshape(128, 2, 64)
            swvs[2 * c + bb] = r["swv_s"][bb].reshape(128, 2, 64)
        if j == 1:
            swkp[b] = r["swk_p"].reshape(128, 2, 64)
            swvp[b] = r["swv_p"].reshape(128, 2, 64)
        else:
            mkp[:, b] = r["mk_p"].reshape(2, 256, 4, 128)
            mvp[:, b] = r["mv_p"].reshape(2, 256, 4, 128)
    return (y_prompt, y_sample, dkp, dvp, dks, dvs, swkp, swvp, swks, swvs, mkp, mvp)
```

```python
import math
import numpy as np
import ml_dtypes
import concourse.bass as bass
import concourse.mybir as mybir
from concourse.bass_utils import run_bass_kernel_spmd

F32 = mybir.dt.float32
BF16 = mybir.dt.bfloat16
AF = mybir.ActivationFunctionType
ALU = mybir.AluOpType
AX = mybir.AxisListType

N_DMA_SEMS = 24
NT = 32
OWN0 = 16
T0 = 15
TS = 64
LN_EPS = 1e-5
ALPHA = 4.0 ** 0.25
LAM_INIT = 0.2


class Emitter:
    COMPUTE = ("pe", "act", "dve", "pool")

    def __init__(self, nc):
        self.nc = nc
        self.engs = {"pe": nc.tensor, "act": nc.scalar, "dve": nc.vector,
                     "pool": nc.gpsimd, "sp": nc.sync}
        self.esem = {e: nc.alloc_semaphore(name=f"es_{e}") for e in self.COMPUTE}
        self.dsem = [nc.alloc_semaphore(name=f"ds_{i}") for i in range(N_DMA_SEMS)]
        self.ops = []
        self.state = {}
        self.dma_last = [None] * N_DMA_SEMS
        self.dma_rr = 0
        self.dma_rr_sw = 0

    def _deps(self, reads, writes, eng=None):
        deps = set()
        for r in reads:
            st = self.state.get(r)
            if st and st[0] is not None:
                deps.add(st[0])
            if st and r in self.PSUM_RES:
                for e2, oid in st[1].items():
                    if e2 != eng:
                        deps.add(oid)
        for w in writes:
            st = self.state.get(w)
            if st:
                if st[0] is not None:
                    deps.add(st[0])
                deps.update(st[1].values())
        return deps

    def _record(self, oid, eng, reads, writes):
        for r in reads:
            st = self.state.setdefault(r, [None, {}])
            st[1][eng] = oid
        for w in writes:
            self.state[w] = [oid, {}]

    PSUM_RES = frozenset(["A0", "A1", "B0", "B1", "O0", "O1", "P0", "P1"])

    def op(self, eng, fn, reads=(), writes=()):
        deps = self._deps(reads, writes, eng)
        oid = len(self.ops)
        self.ops.append(dict(eng=eng, fn=fn, deps=deps, dma=None))
        self._record(oid, eng, reads, writes)
        return oid

    def dma(self, out, in_, reads=(), writes=(), eng="sp"):
        deps = self._deps(reads, writes)
        half = N_DMA_SEMS // 2
        if eng == "pool":
            k = half + self.dma_rr_sw
            self.dma_rr_sw = (self.dma_rr_sw + 1) % (N_DMA_SEMS - half)
        else:
            k = self.dma_rr
            self.dma_rr = (self.dma_rr + 1) % half
        if self.dma_last[k] is not None:
            deps.add(self.dma_last[k])
        oid = len(self.ops)
        self.ops.append(dict(eng=eng, fn=lambda e: e.dma_start(out=out, in_=in_),
                             deps=deps, dma=k))
        self.dma_last[k] = oid
        self._record(oid, ("dma", oid), reads, writes)
        return oid

    def finalize(self):
        ops = self.ops

        def skip(p, o):
            return p["dma"] is None and o["dma"] is None and p["eng"] == "pe" and o["eng"] == "pe"

        needed = [False] * len(ops)
        for o in ops:
            for d in o["deps"]:
                if not skip(ops[d], o):
                    needed[d] = True
        cnt = {e: 0 for e in self.COMPUTE}
        dcnt = [0] * N_DMA_SEMS
        val = [None] * len(ops)
        waited = {}
        for i, o in enumerate(ops):
            eng = o["eng"]
            e = self.engs[eng]
            want = {}
            for d in o["deps"]:
                if skip(ops[d], o):
                    continue
                key, v = val[d]
                if want.get(key, 0) < v:
                    want[key] = v
            for key, v in want.items():
                if waited.get((eng, key), 0) >= v:
                    continue
                waited[(eng, key)] = v
                sem = self.esem[key] if isinstance(key, str) else self.dsem[key]
                e.wait_ge(sem, v)
            ins = o["fn"](e)
            if o["dma"] is not None:
                k = o["dma"]
                dcnt[k] += 16
                ins.then_inc(self.dsem[k], 16)
                val[i] = (k, dcnt[k])
            elif needed[i]:
                cnt[eng] += 1
                ins.then_inc(self.esem[eng], 1)
                val[i] = (eng, cnt[eng])
        sp = self.engs["sp"]
        for k in range(N_DMA_SEMS):
            if dcnt[k] and waited.get(("sp", k), 0) < dcnt[k]:
                sp.wait_ge(self.dsem[k], dcnt[k])
        return dict(n_ops=len(ops), cnt=cnt)


class Rot:
    def __init__(self, nc, name, shape, dtype, n=1):
        self.bufs = [nc.alloc_sbuf_tensor(f"r_{name}{i}", shape, dtype) for i in range(n)]
        self.names = [f"{name}{i}" for i in range(n)]
        self.i = -1

    def next(self):
        self.i = (self.i + 1) % len(self.bufs)
        return self.bufs[self.i], self.names[self.i]


def build_program():
    nc = bass.Bass("TRN2", target_bir_lowering=False)
    em = Emitter(nc)

    def din(name, shape, dt=F32):
        return nc.dram_tensor(name, list(shape), dt, kind="ExternalInput").ap()

    def dout(name, shape):
        return nc.dram_tensor(name, list(shape), F32, kind="ExternalOutput").ap()

    xloc = din("xloc", [NT * 128, 1024])
    xs_d = din("xs", [TS, 1024])
    memp = din("memp", [256, 1024])
    cdk = din("cdk", [2, 4096, 512])
    cdv = din("cdv", [2, 4096, 512])
    cswk = din("cswk", [2, 128, 128])
    cswv = din("cswv", [2, 128, 128])
    cmk = din("cmk", [2, 2, 256, 512])
    cmv = din("cmv", [2, 2, 256, 512])
    w_in_a = din("w_in_a", [1024, 3072])
    w_in_b = din("w_in_b", [1024, 2048])
    w_kv = din("w_kv", [1024, 256])
    w_mem = din("w_mem", [2, 1024, 1024])
    w_out = din("w_out", [2, 1024, 1024])
    lamv = din("lamv", [4, 64])
    dng = din("dng", [1, 128])
    sinks_d = din("sinks", [1, 8])
    lng = din("lng", [2, 1024])
    lnb = din("lnb", [2, 1024])
    cosP_d = din("cosP", [128, NT, 8])
    sinP_d = din("sinP", [128, NT, 8])
    cosS_d = din("cosS", [TS, 8])
    sinS_d = din("sinS", [TS, 8])
    pflag_d = din("pflag", [128, 1])
    ident_d = din("ident", [128, 128], BF16)

    y_p = dout("y_p", [2048, 1024])
    dk_p = dout("dk_p", [2048, 512])
    dv_p = dout("dv_p", [2048, 512])
    y_s = dout("y_s", [TS, 1024])
    dk_s = dout("dk_s", [TS, 512])
    dv_s = dout("dv_s", [TS, 512])
    swk_p = dout("swk_p", [128, 128])
    swv_p = dout("swv_p", [128, 128])
    swk_s = dout("swk_s", [2, 128, 128])
    swv_s = dout("swv_s", [2, 128, 128])
    mk_p = dout("mk_p", [2, 256, 512])
    mv_p = dout("mv_p", [2, 256, 512])
    scr = nc.dram_tensor("scr", [(NT - T0) * 128, 1024], F32, kind="Internal").ap()

    def sb(n, shp, dt):
        return nc.alloc_sbuf_tensor("s_" + n, shp, dt)
    WR = sb("WR", [128, 8, 4096], BF16)
    KT = sb("KT", [128, 4, (NT + 1) * 128], BF16)
    VA = sb("VA", [128, NT + 1, 4, 129], BF16)
    _lng = sb("lngt", [128, 1024], F32)
    _lnb = sb("lnbt", [128, 1024], F32)
    lngt = [_lng, _lng]
    lnbt = [_lnb, _lnb]
    ident = sb("ident", [128, 128], BF16)
    cosP = sb("cosP", [128, NT, 8], F32)
    sinP = sb("sinP", [128, NT, 8], F32)
    cosS = sb("cosS", [128, 8], F32)
    sinS = sb("sinS", [128, 8], F32)
    pflag = sb("pflag", [128, 1], F32)
    lams = sb("lams", [128, 8], F32)
    gtab = sb("gtab", [128, 128], F32)
    esink = sb("esink", [128, 8], F32)
    cneg = sb("cneg", [128, 8], F32)
    mkTs = sb("mkTs", [128, 4, 256], BF16)
    mvas = sb("mvas", [128, 2, 4, 129], BF16)
    _mkT = sb("mkT0", [128, 4, 256], BF16)
    _mva = sb("mva0", [128, 2, 4, 129], BF16)
    mkT = [_mkT, mkTs]
    mva = [_mva, mvas]
    mkT1_scr = nc.dram_tensor("mkT1_scr", [128, 1024], BF16, kind="Internal").ap()
    mva1_scr = nc.dram_tensor("mva1_scr", [128, 1032], BF16, kind="Internal").ap()

    xbR = Rot(nc, "xb", [128, 1024], BF16, 2)
    xTR = Rot(nc, "xT", [128, 8, 128], BF16, 2)
    hR = Rot(nc, "hb", [128, 1024], F32, 2)
    f512R = Rot(nc, "f5", [128, 512], F32, 2)
    sgR = Rot(nc, "sg", [128, 512], F32, 4)
    b512R = Rot(nc, "b5", [128, 512], BF16, 2)
    qTR = Rot(nc, "qT", [128, 4, 128], BF16, 4)
    ER = Rot(nc, "E", [128, 2, 4, 128], BF16, 2)
    mixR = Rot(nc, "mix", [128, 1024], BF16, 1)
    mixTR = xTR
    OsbR = Rot(nc, "Osb", [128, 4, 2, 129], F32, 1)
    OmsbR = Rot(nc, "Omsb", [128, 4, 129], F32, 1)
    w512R = Rot(nc, "w5", [128, 512], F32, 2)
    smR = Rot(nc, "sm", [128, 32], F32, 4)
    ropeR = Rot(nc, "rp", [128, 4, 64], F32, 2)
    memb = KT[:, 0, 0:2048].rearrange("p (a b) -> p a b", a=2)
    memT = KT[:, 1, 0:2048].rearrange("p (a b) -> p a b", a=8)
    xs1 = sb("xs1", [128, 1024], F32)
    mixs = sb("mixs", [128, 1024], BF16)
    mixPR = Rot.__new__(Rot)
    mixPR.bufs = [mixR.bufs[0], mixs]
    mixPR.names = [mixR.names[0], "mixs"]
    mixPR.i = -1
    KT0 = KT[:, 0, :]
    _off = [0]

    def carve(n, shape3):
        a = KT0[:, _off[0]:_off[0] + n].rearrange("p (a b) -> p a b", a=shape3[0])
        _off[0] += n
        return a

    class RotV:
        def __init__(self, name, n, cnt, shape3):
            self.bufs = [carve(cnt, shape3) for _ in range(n)]
            self.names = [f"v_{name}{i}" for i in range(n)]
            self.i = -1

        def next(self):
            self.i = (self.i + 1) % len(self.bufs)
            return self.bufs[self.i], self.names[self.i]

    K2R = RotV("K2", 5, 256, (2, 128))
    VsR = RotV("Vs", 5, 130, (2, 65))
    kdupR = RotV("kdup", 3, 256, (2, 128))
    K2C = carve(256, (2, 128))
    cswb = carve(256, (2, 128))
    VmD = [carve(130, (2, 65)) for _ in range(2)]
    KT1 = KT[:, 1, :]
    KT2 = KT[:, 2, :]

    class RotX:
        def __init__(self, base, name, n):
            self.bufs = [base[:, i * 1024:(i + 1) * 1024].rearrange("p (a b) -> p a b", a=8) for i in range(n)]
            self.names = [f"v_{name}{i}" for i in range(n)]
            self.i = -1

        def next(self):
            self.i = (self.i + 1) % len(self.bufs)
            return self.bufs[self.i], self.names[self.i]

    xT2R = RotX(KT1, "xT2", 4)
    mixT2R = RotX(KT2, "mixT2", 2)
    L1V_NAMES = K2R.names + VsR.names + kdupR.names + ["K2C", "cswb", "VmD0", "VmD1"] + xT2R.names + mixT2R.names

    psA = nc.alloc_psum_tensor("psA", [128, 1024], F32)
    psB = nc.alloc_psum_tensor("psB", [128, 1024], F32)
    psO = nc.alloc_psum_tensor("psO", [128, 1024], F32)
    psP = [nc.alloc_psum_tensor(f"psP{i}", [128, 512], F32) for i in range(2)]
    Ssets = [(psA, ("A0", "A1")), (psB, ("B0", "B1"))]
    sidx = [0]

    S_SINGLE = [False]

    def next_S():
        if S_SINGLE[0]:
            return Ssets[0]
        sidx[0] ^= 1
        return Ssets[sidx[0]]

    pidx = [0]
    P_SMALL = [(psP[0], "P0"), (psP[1], "P1")]
    P_ALL = P_SMALL + [(psA[:, 0:512], "A0"), (psA[:, 512:1024], "A1"), (psB[:, 0:512], "B0"),
                       (psB[:, 512:1024], "B1"), (psO[:, 0:512], "O0"), (psO[:, 512:1024], "O1")]
    P_POOL = [P_ALL]

    def next_P():
        pidx[0] = (pidx[0] + 1) % len(P_POOL[0])
        return P_POOL[0][pidx[0]]

    ORES = ("O0", "O1")

    def WRr(c0, c1):
        return [("WR", b) for b in range(c0 // 512, (c1 + 511) // 512)]

    em.dma(ident[:], ident_d, writes=["ident"])
    em.dma(cosP[:], cosP_d, writes=["cosP"])
    em.dma(sinP[:], sinP_d, writes=["sinP"])
    em.dma(cosS[0:TS, :], cosS_d, writes=["cosS"])
    em.dma(sinS[0:TS, :], sinS_d, writes=["sinS"])
    em.dma(pflag[:], pflag_d, writes=["pflag"])
    lamt_b, lamt_n = f512R.next()
    lamt = lamt_b[:, 0:256].rearrange("p (a b) -> p a b", a=4)
    em.dma(lamt_b[:, 0:256], lamv.rearrange("a b -> (a b)").partition_broadcast(128), writes=[lamt_n])
    em.dma(gtab[:, :], dng[0, :].partition_broadcast(128), writes=["gtab"])
    em.dma(esink[:], sinks_d[0, :].partition_broadcast(128), writes=["esink"])
    def load_ln(l):
        em.dma(lngt[l][:], lng[l, :].partition_broadcast(128), writes=["lng"])
        em.dma(lnbt[l][:], lnb[l, :].partition_broadcast(128), writes=["lnb"])
    load_ln(0)
    em.op("dve", lambda e: e.memset(cneg[:], -0.5), writes=["cneg"])
    w, wn = w512R.next()
    em.op("dve", lambda e: e.tensor_tensor(w[:, 0:64], lamt[:, 0, :], lamt[:, 1, :], ALU.mult), reads=[lamt_n], writes=[wn])
    em.op("dve", lambda e: e.tensor_tensor(w[:, 64:128], lamt[:, 2, :], lamt[:, 3, :], ALU.mult), reads=[lamt_n, wn], writes=[wn])
    em.op("dve", lambda e: e.reduce_sum(lams[:, 4:6], w[:, 0:128].rearrange("p (a b) -> p a b", a=2), axis=AX.X), reads=[wn], writes=["lams"])
    em.op("act", lambda e: e.activation(lams[:, 0:2], lams[:, 4:6], AF.Exp), reads=["lams"], writes=["lams"])
    em.op("dve", lambda e: e.tensor_tensor(lams[:, 2:3], lams[:, 0:1], lams[:, 1:2], ALU.subtract), reads=["lams"], writes=["lams"])
    em.op("dve", lambda e: e.tensor_scalar(lams[:, 3:4], lams[:, 2:3], -1.0, -LAM_INIT, ALU.mult, ALU.add), reads=["lams"], writes=["lams"])
    neg_lam = lams[:, 3:4]
    em.op("dve", lambda e: e.tensor_scalar(gtab[:, :], gtab[:, :], (1.0 - LAM_INIT) * 0.5, None, ALU.mult), reads=["gtab"], writes=["gtab"])
    em.op("act", lambda e: e.activation(esink[:], esink[:], AF.Exp), reads=["esink"], writes=["esink"])

    CP = ["act"]
    MIXENG = ["dve"]
    LNENG = ["dve"]
    XTENG = ["dve"]

    def evac(dst, src, reads, writes, eng=None):
        eng = eng or CP[0]
        if eng == "act":
            em.op("act", lambda e: e.copy(dst, src), reads=reads, writes=writes)
        else:
            em.op("dve", lambda e: e.tensor_copy(dst, src), reads=reads, writes=writes)

    def run(g):
        try:
            while True:
                next(g)
        except StopIteration as stop:
            return stop.value

    def pipeline(gens, depth):
        it = iter(gens)
        live = []
        more = True
        while True:
            if more and len(live) < depth:
                g = next(it, None)
                if g is None:
                    more = False
                else:
                    live.append(g)
            if not live:
                break
            for g in list(live):
                try:
                    next(g)
                except StopIteration:
                    live.remove(g)

    def weave_spread(main, n_main, others, n_others):
        others = [g for g in others if g is not None]
        k = max(1, n_main // max(1, n_others))
        oi = 0
        done = False
        while not done or others:
            for _ in range(k):
                if not done:
                    try:
                        next(main)
                    except StopIteration:
                        done = True
            if others:
                g = others[oi % len(others)]
                try:
                    next(g)
                    oi += 1
                except StopIteration:
                    others.remove(g)

    def delayed(g, n):
        if g is None:
            return
        for _ in range(n):
            yield
        yield from g

    def weave(*gens):
        live = [g for g in gens if g is not None]
        while live:
            for g in list(live):
                try:
                    next(g)
                except StopIteration:
                    live.remove(g)

    def transposes(src_fn, n, T, dst_flat, dstres, srcres, eng="dve"):
        if eng == "cp":
            eng = CP[0]
        pb, pres = next_P()
        pT = pb.bitcast(BF16)
        for j in range(n):
            em.op("pe", lambda e, j=j: e.transpose(pT[:, j * T:(j + 1) * T], src_fn(j), ident[0:T, 0:T]),
                  reads=list(srcres) + ["ident"], writes=[pres])
        src3 = pT[:, 0:n * T].rearrange("p (a b) -> p a b", a=n)
        if eng == "act":
            em.op("act", lambda e: e.copy(dst_flat, src3), reads=[pres], writes=list(dstres))
        else:
            em.op("dve", lambda e: e.tensor_copy(dst_flat, src3), reads=[pres], writes=list(dstres))

    def make_xT(xb, xbn, T, rot=None):
        xT, xTn = (rot or xTR).next()
        transposes(lambda j: xb[0:T, j * 128:(j + 1) * 128], 8, T, xT[:, :, 0:T], [xTn], [xbn], eng=XTENG[0])
        return xT, xTn

    def proj(xT, xTn, T, c0, N=512):
        pb, pres = next_P()
        for ch in range(8):
            em.op("pe", lambda e, ch=ch: e.matmul(pb[0:T, 0:N], xT[:, ch, 0:T], WR[:, ch, c0:c0 + N],
                                                   start=(ch == 0), stop=(ch == 7)),
                  reads=[xTn] + WRr(c0, c0 + N), writes=[pres])
        return pb, pres

    def rope(pb, pres, T, dst, dstn, cos, sin, tabres, ncol=512):
        g = ncol // 64
        src = pb[0:T, 0:ncol].rearrange("p (g d) -> p g d", g=g)
        d3 = dst[0:T, 0:ncol].rearrange("p (g d) -> p g d", g=g)
        cb = cos.unsqueeze(1).to_broadcast([T, g, 8])
        sbb = sin.unsqueeze(1).to_broadcast([T, g, 8])
        rp, rpn = ropeR.next()
        tt = [rp[0:T, i, 0:g * 8].rearrange("p (g d) -> p g d", g=g) for i in range(4)]
        tn = [(rpn, i) for i in range(4)]
        evac(dst[0:T, 0:ncol], pb[0:T, 0:ncol], [pres], [dstn])
        em.op("dve", lambda e: e.tensor_tensor(tt[0], src[:, :, 0:8], cb, ALU.mult), reads=[pres] + tabres, writes=[tn[0]])
        em.op("dve", lambda e: e.tensor_tensor(tt[1], src[:, :, 8:16], sbb, ALU.mult), reads=[pres] + tabres, writes=[tn[1]])
        em.op("dve", lambda e: e.tensor_tensor(tt[2], src[:, :, 8:16], cb, ALU.mult), reads=[pres] + tabres, writes=[tn[2]])
        em.op("dve", lambda e: e.tensor_tensor(tt[3], src[:, :, 0:8], sbb, ALU.mult), reads=[pres] + tabres, writes=[tn[3]])
        em.op("dve", lambda e: e.tensor_tensor(d3[:, :, 0:8], tt[0], tt[1], ALU.subtract), reads=[tn[0], tn[1], dstn], writes=[dstn])
        em.op("dve", lambda e: e.tensor_tensor(d3[:, :, 8:16], tt[2], tt[3], ALU.add), reads=[tn[2], tn[3], dstn], writes=[dstn])

    def gate(pb, pres, T):
        f, fn = f512R.next()
        sg, sgn = sgR.next()
        em.op("act", lambda e: e.activation(f[0:T, :], pb[0:T, 0:512], AF.Tanh, scale=0.5), reads=[pres], writes=[fn])
        em.op("dve", lambda e: e.scalar_tensor_tensor(sg[0:T, :], f[0:T, :], 1.0, pb[0:T, 0:512], ALU.add, ALU.mult),
              reads=[fn, pres], writes=[sgn])
        return sg, sgn

    def to_T4(pb, pres, T):
        b, bn = b512R.next()
        evac(b[0:T, :], pb[0:T, 0:512], [pres], [bn])
        return b_to_T4(b, bn, T)

    def b_to_T4(b, bn, T):
        qT, qTn = qTR.next()
        transposes(lambda j: b[0:T, j * 128:(j + 1) * 128], 4, T, qT[:, :, 0:T], [qTn], [bn], eng="cp")
        return qT, qTn

    def rsqrt_small(dst, src_ap, n, T, scale, resr, resw):
        em.op("dve", lambda e: e.tensor_scalar(dst, src_ap, scale, LN_EPS, ALU.mult, ALU.add), reads=resr, writes=resw)
        em.op("pool", lambda e: e.tensor_tensor(dst, dst, cneg[0:T, 0:n], ALU.pow), reads=resw + ["cneg"], writes=resw)

    def diff_attn(QT, QTn, T, nkt, diag, kts=None, acc=None):
        if kts is None:
            kts = list(range(nkt))
        kfirst, klast = kts[0], kts[-1]
        Osb, Osbn = acc if acc is not None else OsbR.next()
        for h in range(4):
            groups = [kts[g:g + 4] for g in range(0, len(kts), 4)]
            SE = []

            def qk(grp):
                S, Sres = next_S()
                for i, kt in enumerate(grp):
                    for c in range(2):
                        em.op("pe", lambda e, S=S, i=i, kt=kt, c=c, h=h: e.matmul(
                            S[:, c * 512 + i * 128: c * 512 + i * 128 + T],
                            KT[c * 64:(c + 1) * 64, h, kt * 128:(kt + 1) * 128],
                            QT[c * 64:(c + 1) * 64, h, 0:T], start=True, stop=True),
                            reads=[("KT", kt), QTn], writes=[Sres[c]])
                return S, Sres

            def pv(E, En, grp):
                for c in range(2):
                    for i, kt in enumerate(grp):
                        em.op("pe", lambda e, E=E, i=i, kt=kt, c=c, h=h: e.matmul(
                            psO[0:T, c * 512: c * 512 + 129], E[:, c, i, 0:T], VA[:, kt, h, :],
                            start=(kt == kfirst), stop=(kt == klast)),
                            reads=[(En, c), ("VA", kt)], writes=[ORES[c]])

            nxt = qk(groups[0])
            pending = None
            for gi, grp in enumerate(groups):
                S, Sres = nxt
                if gi + 1 < len(groups):
                    nxt = qk(groups[gi + 1])
                E, En = ER.next()
                n = len(grp)
                S4 = S[:].rearrange("p (c k q) -> p c k q", c=2, k=4)
                for c in range(2):
                    em.op("act", lambda e, S4=S4, E=E, n=n, c=c: e.activation(
                        E[:, c, 0:n, 0:T], S4[:, c, 0:n, 0:T], AF.Exp, scale=0.125), reads=[Sres[c]], writes=[(En, c)])
                    if diag is not None and diag in grp:
                        i = grp.index(diag)
                        em.op("act", lambda e, E=E, i=i, S4=S4, c=c: e.activation(
                            E[64:128, c, i, 0:64], S4[64:128, c, i, 0:64], AF.Identity, scale=0.0),
                            reads=[Sres[c], (En, c)], writes=[(En, c)])
                yield
                if pending is not None:
                    pv(*pending)
                pending = (E, En, grp)
            pv(*pending)
            if acc is None:
                em.op("dve", lambda e, h=h: e.tensor_copy(Osb[0:T, h, 0, :], psO[0:T, 0:129]), reads=[ORES[0]], writes=[Osbn])
                evac(Osb[0:T, h, 1, :], psO[0:T, 512:641], [ORES[1]], [Osbn])
            else:
                em.op("dve", lambda e, h=h: e.tensor_tensor(Osb[0:T, h, 0, :], Osb[0:T, h, 0, :], psO[0:T, 0:129], ALU.add),
                      reads=[ORES[0], Osbn], writes=[Osbn])
                em.op("dve", lambda e, h=h: e.tensor_tensor(Osb[0:T, h, 1, :], Osb[0:T, h, 1, :], psO[0:T, 512:641], ALU.add),
                      reads=[ORES[1], Osbn], writes=[Osbn])
        return Osb, Osbn

    def diff_post(Osb, Osbn, T, sg, sgn, mix, mixn):
        sm, smn = smR.next()
        w1, w1n = w512R.next()
        w2, w2n = w512R.next()
        rz = sm[0:T, 0:8].rearrange("p (h c) -> p h c", c=2)
        em.op("dve", lambda e: e.tensor_scalar(rz, Osb[0:T, :, :, 128], 1e-30, None, ALU.add), reads=[Osbn], writes=[smn])
        em.op("dve", lambda e: e.reciprocal(rz, rz), reads=[smn], writes=[smn])
        em.op("dve", lambda e: e.tensor_scalar(sm[0:T, 8:12], rz[:, :, 1], neg_lam[0:T, :], None, ALU.mult), reads=[smn, "lams"], writes=[smn])
        w1v = w1[0:T, :].rearrange("p (h d) -> p h d", h=4)
        w2v = w2[0:T, :].rearrange("p (h d) -> p h d", h=4)
        em.op("dve", lambda e: e.tensor_tensor(w1v, Osb[0:T, :, 0, 0:128], rz[:, :, 0:1].to_broadcast([T, 4, 128]), ALU.mult), reads=[Osbn, smn], writes=[w1n])
        em.op("dve", lambda e: e.tensor_tensor(w2v, Osb[0:T, :, 1, 0:128], sm[0:T, 8:12].unsqueeze(2).to_broadcast([T, 4, 128]), ALU.mult), reads=[Osbn, smn], writes=[w2n])
        em.op("dve", lambda e: e.tensor_tensor(w1[0:T, :], w1[0:T, :], w2[0:T, :], ALU.add), reads=[w1n, w2n], writes=[w1n])
        em.op("dve", lambda e: e.tensor_tensor(w2[0:T, :], w1[0:T, :], w1[0:T, :], ALU.mult), reads=[w1n, w2n], writes=[w2n])
        em.op("dve", lambda e: e.reduce_sum(sm[0:T, 12:16], w2v, axis=AX.X), reads=[w2n, smn], writes=[smn])
        rsqrt_small(sm[0:T, 16:20], sm[0:T, 12:16], 4, T, 1.0 / 128, [smn], [smn])
        em.op("dve", lambda e: e.tensor_tensor(w1v, w1v, sm[0:T, 16:20].unsqueeze(2).to_broadcast([T, 4, 128]), ALU.mult), reads=[w1n, smn], writes=[w1n])
        em.op("dve", lambda e: e.tensor_tensor(w1v, w1v, gtab[0:T, :].unsqueeze(1).to_broadcast([T, 4, 128]), ALU.mult), reads=[w1n, "gtab"], writes=[w1n])
        em.op(MIXENG[0], lambda e: e.tensor_tensor(mix[0:T, 0:512], w1[0:T, :], sg[0:T, :], ALU.mult), reads=[w1n, sgn], writes=[mixn])

    def mem_pe(mqT, mqTn, T, mk, mkn, mv, mvn):
        S, Sres = next_S()
        for h in range(4):
            for kt in range(2):
                i = h * 2 + kt
                em.op("pe", lambda e, h=h, kt=kt, i=i: e.matmul(
                    S[:, i * 128: i * 128 + T], mk[:, h, kt * 128:(kt + 1) * 128], mqT[:, h, 0:T],
                    start=True, stop=True), reads=[mkn, mqTn], writes=[Sres[i // 4]])
        E, En = ER.next()
        Ef = E[:].rearrange("p a b c -> p (a b) c")
        em.op("act", lambda e: e.activation(Ef[:, :, 0:T], S[:].rearrange("p (k q) -> p k q", k=8)[:, :, 0:T],
                                            AF.Exp, scale=128.0 ** -0.5), reads=list(Sres), writes=[(En, 0), (En, 1)])
        yield
        for h in range(4):
            for kt in range(2):
                em.op("pe", lambda e, h=h, kt=kt: e.matmul(
                    psO[0:T, (h // 2) * 512 + (h % 2) * 129: (h // 2) * 512 + (h % 2) * 129 + 129],
                    Ef[:, h * 2 + kt, 0:T], mv[:, kt, h, :], start=(kt == 0), stop=(kt == 1)),
                    reads=[(En, 0), (En, 1), mvn], writes=[ORES[h // 2]])
        Om, Omn = OmsbR.next()
        evac(Om[0:T, 0:2, :], psO[0:T, 0:258].rearrange("p (h d) -> p h d", h=2), [ORES[0]], [Omn])
        evac(Om[0:T, 2:4, :], psO[0:T, 512:770].rearrange("p (h d) -> p h d", h=2), [ORES[1]], [Omn])
        yield
        return Om, Omn

    def mem_post(Om, Omn, T, sg, sgn, mix, mixn):
        sm, smn = smR.next()
        w1, w1n = w512R.next()
        em.op("dve", lambda e: e.reciprocal(sm[0:T, 0:4], Om[0:T, :, 128]), reads=[Omn], writes=[smn])
        em.op("dve", lambda e: e.tensor_scalar(sm[0:T, 0:4], sm[0:T, 0:4], 0.5, None, ALU.mult), reads=[smn], writes=[smn])
        w1v = w1[0:T, :].rearrange("p (h d) -> p h d", h=4)
        em.op("dve", lambda e: e.tensor_tensor(w1v, Om[0:T, :, 0:128], sm[0:T, 0:4].unsqueeze(2).to_broadcast([T, 4, 128]), ALU.mult), reads=[Omn, smn], writes=[w1n])
        em.op(MIXENG[0], lambda e: e.tensor_tensor(mix[0:T, 512:1024], w1[0:T, :], sg[0:T, :], ALU.mult), reads=[w1n, sgn], writes=[mixn])

    def swa_pe(QsT, QsTn, T, ktiles, masks):
        EE = []
        for kv in range(2):
            S, Sres = next_S()
            for pl in range(2):
                p = kv * 2 + pl
                for kt in range(2):
                    K2, K2n, Vs, Vsn, nk = ktiles[kt]
                    for ee in range(2):
                        em.op("pe", lambda e, pl=pl, kt=kt, ee=ee, K2=K2, nk=nk, p=p, S=S, kv=kv: e.matmul(
                            S[0:nk, ee * 512 + (pl * 2 + kt) * 128: ee * 512 + (pl * 2 + kt) * 128 + T],
                            K2[ee * 64:(ee + 1) * 64, kv, 0:nk], QsT[ee * 64:(ee + 1) * 64, p, 0:T],
                            start=True, stop=True), reads=[K2n, QsTn], writes=[Sres[ee]])
            E, En = ER.next()
            S5 = S[:].rearrange("p (c a k q) -> p c a k q", c=2, a=2, k=2)
            E5 = E[:].rearrange("p c (a k) q -> p c a k q", a=2)
            for kt in range(2):
                nk = ktiles[kt][4]
                em.op("act", lambda e, kt=kt, nk=nk, S5=S5, E5=E5: e.activation(
                    E5[0:nk, :, :, kt, 0:T], S5[0:nk, :, :, kt, 0:T], AF.Exp, scale=0.125), reads=list(Sres), writes=[(En, 0), (En, 1)])
            if masks:
                em.op("act", lambda e, E5=E5, S5=S5: e.activation(E5[0:64, :, :, 0, 64:128], S5[0:64, :, :, 0, 64:128], AF.Identity, scale=0.0),
                      reads=list(Sres) + [(En, 0), (En, 1)], writes=[(En, 0), (En, 1)])
                em.op("act", lambda e, E5=E5, S5=S5: e.activation(E5[64:128, :, :, 1, 0:64], S5[64:128, :, :, 1, 0:64], AF.Identity, scale=0.0),
                      reads=list(Sres) + [(En, 0), (En, 1)], writes=[(En, 0), (En, 1)])
            EE.append((E, En))
            yield
        for kv in range(2):
            E, En = EE[kv]
            for pl in range(2):
                for ee in range(2):
                    hd = kv * 4 + pl * 2 + ee
                    for kt in range(2):
                        K2, K2n, Vs, Vsn, nk = ktiles[kt]
                        em.op("pe", lambda e, pl=pl, ee=ee, kt=kt, hd=hd, Vs=Vs, nk=nk, E=E, kv=kv: e.matmul(
                            psO[0:T, (hd // 4) * 512 + (hd % 4) * 65: (hd // 4) * 512 + (hd % 4) * 65 + 65],
                            E[0:nk, ee, pl * 2 + kt, 0:T], Vs[0:nk, kv, :], start=(kt == 0), stop=(kt == 1)),
                            reads=[(En, 0), (En, 1), Vsn], writes=[ORES[hd // 4]])
            if kv == 0:
                yield
        Osb_, Osn = OsbR.next()
        Os = Osb_[:].rearrange("p a b c -> p (a b c)")[:, 0:520].rearrange("p (h d) -> p h d", h=8)
        evac(Os[0:T, 0:4, :], psO[0:T, 0:260].rearrange("p (h d) -> p h d", h=4), [ORES[0]], [Osn])
        evac(Os[0:T, 4:8, :], psO[0:T, 512:772].rearrange("p (h d) -> p h d", h=4), [ORES[1]], [Osn])
        yield
        return Os, Osn

    def swa_post(Os, Osn, T, sg, sgn, mix, mixn):
        sm, smn = smR.next()
        w1, w1n = w512R.next()
        em.op("dve", lambda e: e.tensor_tensor(sm[0:T, 0:8], Os[0:T, :, 64], esink[0:T, :], ALU.add), reads=[Osn, "esink"], writes=[smn])
        em.op("dve", lambda e: e.reciprocal(sm[0:T, 8:16], sm[0:T, 0:8]), reads=[smn], writes=[smn])
        em.op("dve", lambda e: e.tensor_scalar(sm[0:T, 8:16], sm[0:T, 8:16], 0.5, None, ALU.mult), reads=[smn], writes=[smn])
        w1v = w1[0:T, :].rearrange("p (h d) -> p h d", h=8)
        em.op("dve", lambda e: e.tensor_tensor(w1v, Os[0:T, :, 0:64], sm[0:T, 8:16].unsqueeze(2).to_broadcast([T, 8, 64]), ALU.mult), reads=[Osn, smn], writes=[w1n])
        em.op(MIXENG[0], lambda e: e.tensor_tensor(mix[0:T, 0:512], w1[0:T, :], sg[0:T, :], ALU.mult), reads=[w1n, sgn], writes=[mixn])

    def merge_ln(mix, mixn, T, wc0, hb, hbn, l, dst, dstn, rot=None):
        mT, mTn = (rot or mixTR).next()
        transposes(lambda j: mix[0:T, j * 128:(j + 1) * 128], 8, T, mT[:, :, 0:T], [mTn], [mixn], eng="cp")
        yield
        for blk in range(2):
            pb, pres = next_P()
            for ch in range(8):
                em.op("pe", lambda e, blk=blk, ch=ch, pb=pb: e.matmul(
                    pb[0:T, 0:512], mT[:, ch, 0:T], WR[:, ch, wc0 + blk * 512: wc0 + (blk + 1) * 512],
                    start=(ch == 0), stop=(ch == 7)), reads=[mTn] + WRr(wc0 + blk * 512, wc0 + (blk + 1) * 512), writes=[pres])
            em.op("dve", lambda e, blk=blk, pb=pb: e.scalar_tensor_tensor(
                hb[0:T, blk * 512:(blk + 1) * 512], hb[0:T, blk * 512:(blk + 1) * 512], ALPHA,
                pb[0:T, 0:512], ALU.mult, ALU.add), reads=[hbn, pres], writes=[hbn])
            yield
        sm, smn = smR.next()
        for blk in range(2):
            em.op("dve", lambda e, blk=blk: e.bn_stats(sm[0:T, blk * 6:(blk + 1) * 6], hb[0:T, blk * 512:(blk + 1) * 512]), reads=[hbn, smn], writes=[smn])
        em.op("dve", lambda e: e.bn_aggr(sm[0:T, 12:14], sm[0:T, 0:12].rearrange("p (a b) -> p a b", a=2)), reads=[smn], writes=[smn])
        rsqrt_small(sm[0:T, 14:15], sm[0:T, 13:14], 1, T, 1.0, [smn], [smn])
        em.op("dve", lambda e: e.scalar_tensor_tensor(sm[0:T, 15:16], sm[0:T, 12:13], -1.0, sm[0:T, 14:15], ALU.mult, ALU.mult), reads=[smn], writes=[smn])
        if CP[0] == "act":
            em.op("act", lambda e: e.activation(hb[0:T, :], hb[0:T, :], AF.Identity, bias=sm[0:T, 15:16], scale=sm[0:T, 14:15]), reads=[hbn, smn], writes=[hbn])
        else:
            em.op("dve", lambda e: e.tensor_scalar(hb[0:T, :], hb[0:T, :], sm[0:T, 14:15], sm[0:T, 15:16], ALU.mult, ALU.add), reads=[hbn, smn], writes=[hbn])
        em.op(LNENG[0], lambda e: e.tensor_tensor(hb[0:T, :], hb[0:T, :], lngt[l][0:T, :], ALU.mult), reads=[hbn, "lng"], writes=[hbn])
        em.op(LNENG[0], lambda e: e.tensor_tensor(dst[0:T, :], hb[0:T, :], lnbt[l][0:T, :], ALU.add), reads=[hbn, "lnb"], writes=[dstn])

    def load_w(c0, src, ncols):
        for cc in range(0, ncols, 512):
            n = min(512, ncols - cc)
            em.dma(WR[:, :, c0 + cc: c0 + cc + n], src[:, cc:cc + n].rearrange("(ch p) n -> p ch n", p=128),
                   writes=WRr(c0 + cc, c0 + cc + n), eng="pool")

    def cached_mem_kv(ksrc, vsrc, mk, mkn, mv, mvn):
        for r in range(2):
            cm_b, cm_bn = b512R.next()
            em.dma(cm_b[:], ksrc[r * 128:(r + 1) * 128, :], writes=[cm_bn], eng="pool")
            pb2, pres2 = next_P()
            pT = pb2.bitcast(BF16)
            for h in range(4):
                em.op("pe", lambda e, h=h, pT=pT, cm_b=cm_b: e.transpose(pT[:, h * 128:(h + 1) * 128], cm_b[:, h * 128:(h + 1) * 128], ident[:]),
                      reads=[cm_bn, "ident"], writes=[pres2])
            em.op("dve", lambda e, r=r, pT=pT: e.tensor_copy(mk[:, :, r * 128:(r + 1) * 128], pT[:, 0:512].rearrange("p (a b) -> p a b", a=4)),
                  reads=[pres2], writes=[mkn])
            em.dma(mv[:, r, :, 0:128], vsrc[r * 128:(r + 1) * 128, :].rearrange("p (h d) -> p h d", h=4), writes=[mvn], eng="pool")
        em.op("pool", lambda e: e.memset(mv[:, :, :, 128:129], 1.0), reads=[mvn], writes=[mvn])

    xbA = Rot.__new__(Rot)
    xbA.bufs = xbR.bufs + [mixR.bufs[0], mixs]
    xbA.names = xbR.names + [mixR.names[0], "mixs"]
    xbA.i = -1
    pend = {}

    def prefetch(t):
        if t <= NT and t not in pend:
            xb, xbn = xbA.next()
            if t < NT:
                em.dma(xb[:, :], xloc[t * 128:(t + 1) * 128, :], writes=[xbn], eng="pool")
            else:
                em.dma(xb[0:TS, :], xs_d, writes=[xbn], eng="pool")
            pend[t] = (xb, xbn)

    MEMR = [("KT", t) for t in range(16)]
    em.dma(memb, memp.rearrange("(r p) n -> p r n", p=128), writes=MEMR, eng="pool")
    load_w(0, w_mem[0], 1024)
    load_w(1024, w_mem[1], 1024)
    load_w(2048, w_in_a[:, 512:1536], 1024)
    for t_ in range(4):
        prefetch(t_)
    for r in range(2):
        pb, pres = next_P()
        pT = pb.bitcast(BF16)
        for j in range(8):
            em.op("pe", lambda e, j=j, r=r, pT=pT: e.transpose(pT[:, j * 128:(j + 1) * 128], memb[:, r, j * 128:(j + 1) * 128], ident[:]),
                  reads=MEMR + ["ident"], writes=[pres])
        em.op("dve", lambda e, r=r, pT=pT: e.tensor_copy(memT[:, :, r * 128:(r + 1) * 128], pT[:, 0:1024].rearrange("p (a b) -> p a b", a=8)),
              reads=[pres], writes=["memT"])
    for l in range(2):
        mkn, mvn = ("mkT0", "mva0") if l == 0 else ("mkTs", "mvas")
        for r in range(2):
            for kvi in range(2):
                c0 = l * 1024 + kvi * 512
                pb, pres = next_P()
                for ch in range(8):
                    em.op("pe", lambda e, ch=ch, r=r, c0=c0, pb=pb: e.matmul(
                        pb[:, 0:512], memT[:, ch, r * 128:(r + 1) * 128], WR[:, ch, c0:c0 + 512],
                        start=(ch == 0), stop=(ch == 7)), reads=["memT"] + WRr(c0, c0 + 512), writes=[pres])
                f, fn = f512R.next()
                em.op("act", lambda e, f=f, pb=pb: e.copy(f[:], pb[:, 0:512]), reads=[pres], writes=[fn])
                em.dma((mk_p if kvi == 0 else mv_p)[l, r * 128:(r + 1) * 128, :], f[:], reads=[fn])
                if kvi == 0:
                    b, bn = b512R.next()
                    em.op("dve", lambda e, b=b, pb=pb: e.tensor_copy(b[:], pb[:, 0:512]), reads=[pres], writes=[bn])
                    pb2, pres2 = next_P()
                    pT = pb2.bitcast(BF16)
                    for h in range(4):
                        em.op("pe", lambda e, h=h, b=b, pT=pT: e.transpose(pT[:, h * 128:(h + 1) * 128], b[:, h * 128:(h + 1) * 128], ident[:]),
                              reads=[bn, "ident"], writes=[pres2])
                    em.op("dve", lambda e, l=l, r=r, pT=pT: e.tensor_copy(mkT[l][:, :, r * 128:(r + 1) * 128], pT[:, 0:512].rearrange("p (a b) -> p a b", a=4)),
                          reads=[pres2], writes=[mkn])
                else:
                    em.op("dve", lambda e, l=l, r=r, pb=pb: e.tensor_copy(mva[l][:, r, :, 0:128], pb[:, 0:512].rearrange("p (h d) -> p h d", h=4)),
                          reads=[pres], writes=[mvn])
                    em.op("pool", lambda e, l=l, r=r: e.memset(mva[l][:, r, :, 128:129], 1.0), reads=[mvn], writes=[mvn])
    em.dma(mkT1_scr, mkTs[:].rearrange("p a b -> p (a b)"), reads=["mkTs"], writes=["mkT1_scr"])
    em.dma(mva1_scr, mvas[:].rearrange("p a b c -> p (a b c)"), reads=["mvas"], writes=["mva1_scr"])

    def late_weights():
        load_w(0, w_in_a[:, 0:512], 512)
        load_w(512, w_in_a[:, 1536:3072], 1536)

    def load_xb(row0, T, src):
        xb, xbn = xbR.next()
        em.dma(xb[0:T, :], src[row0:row0 + T, :], writes=[xbn], eng="pool")
        return xb, xbn

    def kv_gen(xb, xbn, T, t, cos, sin, tabres, kdst, vdst, ones_src, extra_w=()):
        xT, xTn = make_xT(xb, xbn, T)
        yield
        pb, pres = proj(xT, xTn, T, 2048)
        kf, kfn = sgR.next()
        rope(pb, pres, T, kf, kfn, cos, sin, tabres)
        if kdst is not None:
            em.dma(kdst, kf[0:T, :], reads=[kfn], writes=["dks_d"] if T == TS else [])
        kb4, kbn = qTR.next()
        kb = kb4[:].rearrange("p a b -> p (a b)")
        em.op("act", lambda e: e.copy(kb[0:T, :], kf[0:T, :]), reads=[kfn], writes=[kbn])
        yield
        pb3, pres3 = proj(xT, xTn, T, 2560)
        if vdst is not None:
            vf, vfn = sgR.next()
            em.op("act", lambda e: e.copy(vf[0:T, :], pb3[0:T, 0:512]), reads=[pres3], writes=[vfn])
            em.dma(vdst, vf[0:T, :], reads=[vfn], writes=["dvs_d"] if T == TS else [])
        em.op("dve", lambda e: e.tensor_copy(VA[0:T, t, :, 0:128], pb3[0:T, 0:512].rearrange("p (h d) -> p h d", h=4)),
              reads=[pres3], writes=[("VA", t)])
        if ones_src is None:
            em.op("pool", lambda e: e.memset(VA[0:T, t, :, 128:129], 1.0), reads=[("VA", t)], writes=[("VA", t)])
        else:
            em.op("pool", lambda e: e.tensor_copy(VA[0:T, t, :, 128:129], ones_src), reads=[("VA", t), "pflag"], writes=[("VA", t)])
        yield
        pb2, pres2 = next_P()
        pT = pb2.bitcast(BF16)
        for h in range(4):
            em.op("pe", lambda e, h=h: e.transpose(pT[:, h * T:(h + 1) * T], kb[0:T, h * 128:(h + 1) * 128], ident[0:T, 0:T]),
                  reads=[kbn, "ident"], writes=[pres2])
        em.op("dve", lambda e: e.tensor_copy(KT[:, :, t * 128: t * 128 + T], pT[:, 0:4 * T].rearrange("p (a b) -> p a b", a=4)),
              reads=[pres2], writes=[("KT", t)] + list(extra_w))
        yield

    def kv_all():
        for t in range(NT):
            own = t >= OWN0
            for a in range(4):
                prefetch(t + a)
            if t == 6:
                late_weights()
            xb_, xbn_ = pend.pop(t)
            yield kv_gen(xb_, xbn_, 128, t, cosP[:, t, :], sinP[:, t, :], ["cosP", "sinP"],
                         dk_p[(t - OWN0) * 128:(t - OWN0 + 1) * 128, :] if own else None,
                         dv_p[(t - OWN0) * 128:(t - OWN0 + 1) * 128, :] if own else None,
                         None if own else pflag[:, 0:1].unsqueeze(1).to_broadcast([128, 4, 1]),
                         extra_w=["memT"] + MEMR if t == 0 else ())
        xb_, xbn_ = pend.pop(NT)
        yield kv_gen(xb_, xbn_, TS, NT, cosS[0:TS, :], sinS[0:TS, :], ["cosS", "sinS"], dk_s, dv_s, None)

    pipeline(kv_all(), 5)
    P_POOL[0] = P_SMALL

    def load_hb(src, row0, T):
        hb, hbn = hR.next()
        em.dma(hb[0:T, :], src[row0:row0 + T, :], writes=[hbn])
        return hb, hbn

    def layer_front(xT, xTn, T, cos, sin, tabres):
        pb, pres = proj(xT, xTn, T, 0)
        qb, qbn = b512R.next()
        rope(pb, pres, T, qb, qbn, cos, sin, tabres)
        yield
        pb, pres = proj(xT, xTn, T, 512)
        sg1, sg1n = gate(pb, pres, T)
        yield
        QT, QTn = b_to_T4(qb, qbn, T)
        pb, pres = proj(xT, xTn, T, 1024)
        mb, mbn = b512R.next()
        evac(mb[0:T, :], pb[0:T, 0:512], [pres], [mbn])
        yield
        pb, pres = proj(xT, xTn, T, 1536)
        sg2, sg2n = gate(pb, pres, T)
        yield
        mqT, mqTn = b_to_T4(mb, mbn, T)
        yield
        return QT, QTn, sg1, sg1n, mqT, mqTn, sg2, sg2n

    ctx = {}

    xbs = {}
    hbs = {}

    def front(t):
        xb, xbn = xbs.pop(t)
        xT, xTn = make_xT(xb, xbn, 128)
        yield
        ctx[t] = yield from layer_front(xT, xTn, 128, cosP[:, t, :], sinP[:, t, :], ["cosP", "sinP"])

    def attn(t):
        QT, QTn, sgd, sgdn, mqT, mqTn, sgm, sgmn = ctx[t]
        Osb, Osbn = yield from diff_attn(QT, QTn, 128, t + 1, t)
        Om, Omn = yield from mem_pe(mqT, mqTn, 128, mkT[0], "mkT0", mva[0], "mva0")
        mix, mixn = mixPR.next()
        diff_post(Osb, Osbn, 128, sgd, sgdn, mix, mixn)
        yield
        mem_post(Om, Omn, 128, sgm, sgmn, mix, mixn)
        ctx[t] = (mix, mixn)
        yield

    def back(t):
        mix, mixn = ctx.pop(t)
        hb, hbn = hbs.pop(t)
        yield from merge_ln(mix, mixn, 128, 3072, hb, hbn, 0, hb, hbn)
        em.dma(scr[(t - T0) * 128:(t - T0 + 1) * 128, :], hb[:], reads=[hbn], writes=[("scr", t)])
        yield

    load_w(3072, w_out[0], 1024)
    load_w(2048, w_out[1], 1024)
    CP[0] = "dve"
    XTENG[0] = "act"
    xbs[T0] = load_xb(T0 * 128, 128, xloc)
    xbs[T0 + 1] = load_xb((T0 + 1) * 128, 128, xloc)
    run(front(T0))
    for t in range(T0, NT):
        if t + 2 < NT:
            xbs[t + 2] = load_xb((t + 2) * 128, 128, xloc)
        hbs[t] = load_hb(xloc, t * 128, 128)
        n_main = 4 * ((t + 1 + 3) // 4) + 4
        weave_spread(attn(t), n_main, [delayed(back(t - 1), 3) if t > T0 else None, front(t + 1) if t + 1 < NT else None], 14)
    run(back(NT - 1))

    xsb, xsbn = load_xb(0, TS, xs_d)
    xsT, xsTn = make_xT(xsb, xsbn, TS)
    QTs, QTsn, sgd_s, sgd_sn, mqTs, mqTsn, sgm_s, sgm_sn = run(layer_front(xsT, xsTn, TS, cosS[0:TS, :], sinS[0:TS, :], ["cosS", "sinS"]))
    load_w(0, w_in_b, 2048)
    em.dma(xs1[0:TS, :], xs_d, writes=["xs1"])

    def fill(bb, half):
        for k4 in range(half * 4, half * 4 + 4):
            stg, stgn = hR.next()
            stgb = stg.bitcast(BF16)
            em.dma(stgb[:, :].rearrange("p (r n) -> p r n", r=4),
                   cdk[bb, k4 * 512:(k4 + 1) * 512, :].rearrange("(r p) n -> p r n", p=128), writes=[stgn], eng="pool")
            em.op("pool", lambda e, k4=k4: e.memset(VA[:, k4 * 4:(k4 + 1) * 4, :, 128:129], 1.0),
                  writes=[("VA", k4 * 4 + r) for r in range(4)])
            for r in range(4):
                em.dma(VA[:, k4 * 4 + r, :, 0:128],
                       cdv[bb, (k4 * 4 + r) * 128:(k4 * 4 + r + 1) * 128, :].rearrange("p (h d) -> p h d", h=4),
                       writes=[("VA", k4 * 4 + r)], eng="pool")
            for r in range(4):
                kt = k4 * 4 + r
                pb2, pres2 = next_P()
                pT = pb2.bitcast(BF16)
                for h in range(4):
                    em.op("pe", lambda e, h=h, r=r, stgb=stgb, pT=pT: e.transpose(
                        pT[:, h * 128:(h + 1) * 128], stgb[:, r * 512 + h * 128: r * 512 + (h + 1) * 128], ident[:]),
                        reads=[stgn, "ident"], writes=[pres2])
                if kt % 2:
                    em.op("dve", lambda e, kt=kt, pT=pT: e.tensor_copy(KT[:, :, kt * 128:(kt + 1) * 128], pT[:, 0:512].rearrange("p (a b) -> p a b", a=4)),
                          reads=[pres2], writes=[("KT", kt)])
                else:
                    em.op("act", lambda e, kt=kt, pT=pT: e.copy(KT[:, :, kt * 128:(kt + 1) * 128], pT[:, 0:512].rearrange("p (a b) -> p a b", a=4)),
                          reads=[pres2], writes=[("KT", kt)])
                if r % 2:
                    yield
        if half == 0:
            return
        r0 = bb * 32
        kb, kbn = b512R.next()
        em.dma(kb[0:TS, :], dk_s, reads=["dks_d"], writes=[kbn], eng="pool")
        pb2, pres2 = next_P()
        pT = pb2.bitcast(BF16)
        for h in range(4):
            em.op("pe", lambda e, h=h, kb=kb, pT=pT: e.transpose(pT[:, h * TS:(h + 1) * TS], kb[0:TS, h * 128:(h + 1) * 128], ident[0:TS, 0:TS]),
                  reads=[kbn, "ident"], writes=[pres2])
        em.op("pool", lambda e: e.memset(KT[:, :, NT * 128:(NT + 1) * 128], 0.0), writes=[("KT", NT)])
        em.op("dve", lambda e, pT=pT: e.tensor_copy(KT[:, :, NT * 128: NT * 128 + TS], pT[:, 0:4 * TS].rearrange("p (a b) -> p a b", a=4)),
              reads=[pres2, ("KT", NT)], writes=[("KT", NT)])
        em.op("pool", lambda e: e.memset(VA[:, NT, :, :], 0.0), writes=[("VA", NT)])
        em.dma(VA[r0:r0 + 16, NT, :, 0:128], dv_s[r0:r0 + 16, :].rearrange("p (h d) -> p h d", h=4), reads=["dvs_d", ("VA", NT)], writes=[("VA", NT)], eng="pool")
        em.op("pool", lambda e, r0=r0: e.memset(VA[r0:r0 + 16, NT, :, 128:129], 1.0), reads=[("VA", NT)], writes=[("VA", NT)])
        yield

    osb0 = {}
    H0 = list(range(0, 16))
    H1 = list(range(16, NT + 1))

    def sattn(bb, half):
        if half == 0:
            osb0[bb] = yield from diff_attn(QTs, QTsn, TS, NT + 1, None, kts=H0)
            return
        Osb, Osbn = yield from diff_attn(QTs, QTsn, TS, NT + 1, None, kts=H1, acc=osb0[bb])
        r0 = bb * 32
        mixt, mixtn = mixR.next()
        diff_post(Osb, Osbn, TS, sgd_s, sgd_sn, mixt, mixtn)
        yield
        cached_mem_kv(cmk[0, bb], cmv[0, bb], mkTs, "mkTs", mvas, "mvas")
        yield
        Om, Omn = yield from mem_pe(mqTs, mqTsn, TS, mkTs, "mkTs", mvas, "mvas")
        mem_post(Om, Omn, TS, sgm_s, sgm_sn, mixt, mixtn)
        em.op("dve", lambda e: e.tensor_copy(mixs[r0:r0 + 32, :], mixt[r0:r0 + 32, :]), reads=[mixtn], writes=["mixs"])
        yield

    run(fill(0, 0))
    weave(fill(0, 1), sattn(0, 0))
    weave(fill(1, 0), sattn(0, 1))
    weave(fill(1, 1), sattn(1, 0))
    run(sattn(1, 1))
    run(merge_ln(mixs, "mixs", TS, 3072, xs1, "xs1", 0, xs1, "xs1"))

    load_w(3072, w_kv, 256)
    load_ln(1)
    em.op("dve", lambda e: e.memset(cneg[:, 4:5], -0.5),
          writes=[("KT", t) for t in range(NT + 1)] + L1V_NAMES)

    def shared_kv(xT, xTn, T, cos, sin, tabres, kout, vout, vmask):
        pb, pres = proj(xT, xTn, T, 3072, N=256)
        kf, kfn = f512R.next()
        rope(pb, pres, T, kf, kfn, cos, sin, tabres, ncol=128)
        em.op("act", lambda e: e.copy(kf[0:T, 128:256], pb[0:T, 128:256]), reads=[pres, kfn], writes=[kfn])
        if kout is not None:
            em.dma(kout, kf[0:T, 0:128], reads=[kfn])
            em.dma(vout, kf[0:T, 128:256], reads=[kfn])
        kd, kdn = kdupR.next()
        kdv = kd[0:T, :, :].rearrange("p k (u d) -> p k u d", u=2)
        for u in range(2):
            em.op("pool", lambda e, u=u: e.tensor_copy(kdv[:, :, u, :], kf[0:T, 0:128].rearrange("p (k d) -> p k d", k=2)),
                  reads=[kfn], writes=[kdn])
        Vs, Vsn = VsR.next()
        if vmask is None:
            evac(Vs[0:T, :, 0:64], kf[0:T, 128:256].rearrange("p (k d) -> p k d", k=2), [kfn], [Vsn])
            em.op("pool", lambda e: e.memset(Vs[0:T, :, 64:65], 1.0), reads=[Vsn], writes=[Vsn])
        else:
            em.op("dve", lambda e: e.tensor_scalar(Vs[0:T, :, 0:64], kf[0:T, 128:256].rearrange("p (k d) -> p k d", k=2), vmask, None, ALU.mult),
                  reads=[kfn, "pflag"], writes=[Vsn])
            em.op("pool", lambda e: e.tensor_copy(Vs[0:T, :, 64:65], vmask.unsqueeze(1).to_broadcast([T, 2, 1])), reads=[Vsn, "pflag"], writes=[Vsn])
        yield
        K2, K2n = K2R.next()
        pb2, pres2 = next_P()
        pT = pb2.bitcast(BF16)
        for kvh in range(2):
            em.op("pe", lambda e, kvh=kvh: e.transpose(pT[:, kvh * T:(kvh + 1) * T], kd[0:T, kvh, :], ident[0:T, 0:T]),
                  reads=[kdn, "ident"], writes=[pres2])
        evac(K2[:, :, 0:T], pT[:, 0:2 * T].rearrange("p (a b) -> p a b", a=2), [pres2], [K2n])
        yield
        return (K2, K2n, Vs, Vsn, T), kf, kfn

    xsb, xsbn = xbR.next()
    em.op("pool", lambda e: e.tensor_copy(xsb[0:TS, :], xs1[0:TS, :]), reads=["xs1"], writes=[xsbn])
    xsT, xsTn = make_xT(xsb, xsbn, TS)
    newkv, kf, kfn = run(shared_kv(xsT, xsTn, TS, cosS[0:TS, :], sinS[0:TS, :], ["cosS", "sinS"], None, None, None))
    K2n_, K2nn, _, _, _ = newkv
    for bb in range(2):
        em.dma(swk_s[bb, 112:128, :], kf[bb * 32: bb * 32 + 16, 0:128], reads=[kfn])
        em.dma(swv_s[bb, 112:128, :], kf[bb * 32: bb * 32 + 16, 128:256], reads=[kfn])
    for bb in range(2):
        r0 = bb * 32
        em.op("dve", lambda e, bb=bb: e.memset(VmD[bb][0:TS, :, :], 0.0), writes=[f"VmD{bb}"])
        em.op("dve", lambda e, bb=bb, r0=r0: e.tensor_copy(VmD[bb][r0:r0 + 16, :, 0:64], kf[r0:r0 + 16, 128:256].rearrange("p (k d) -> p k d", k=2)),
              reads=[kfn, f"VmD{bb}"], writes=[f"VmD{bb}"])
        em.op("dve", lambda e, bb=bb, r0=r0: e.memset(VmD[bb][r0:r0 + 16, :, 64:65], 1.0), reads=[f"VmD{bb}"], writes=[f"VmD{bb}"])
    QTs, QTsn, sgs_s, sgs_sn, mqTs, mqTsn, sgm_s, sgm_sn = run(layer_front(xsT, xsTn, TS, cosS[0:TS, :], sinS[0:TS, :], ["cosS", "sinS"]))
    for bb in range(2):
        r0 = bb * 32
        em.dma(swk_s[bb, 0:112, :], cswk[bb, 16:128, :])
        em.dma(swv_s[bb, 0:112, :], cswv[bb, 16:128, :])
        em.dma(cswb[:, 0, :], cswk[bb], writes=["cswb"], eng="pool")
        em.dma(cswb[:, 1, :], cswv[bb], writes=["cswb"], eng="pool")
        kd, kdn = kdupR.next()
        kdv = kd[:, :, :].rearrange("p k (u d) -> p k u d", u=2)
        for u in range(2):
            em.op("pool", lambda e, u=u, kdv=kdv: e.tensor_copy(kdv[:, :, u, :], cswb[:, 0, :].rearrange("p (k d) -> p k d", k=2)),
                  reads=["cswb"], writes=[kdn])
        pb2, pres2 = next_P()
        pT = pb2.bitcast(BF16)
        for kvh in range(2):
            em.op("pe", lambda e, kvh=kvh, kd=kd, pT=pT: e.transpose(pT[:, kvh * 128:(kvh + 1) * 128], kd[:, kvh, :], ident[:]),
                  reads=[kdn, "ident"], writes=[pres2])
        em.op("dve", lambda e, pT=pT: e.tensor_copy(K2C[:, :, :], pT[:, 0:256].rearrange("p (a b) -> p a b", a=2)), reads=[pres2], writes=["K2C"])
        Vc, Vcn = VsR.next()
        em.op("dve", lambda e, Vc=Vc: e.tensor_copy(Vc[:, :, 0:64], cswb[:, 1, :].rearrange("p (k d) -> p k d", k=2)), reads=["cswb"], writes=[Vcn])
        em.op("pool", lambda e, Vc=Vc: e.memset(Vc[:, :, 64:65], 1.0), reads=[Vcn], writes=[Vcn])
        Vm, Vmn = VmD[bb], f"VmD{bb}"
        mixt, mixtn = mixR.next()
        Os, Osn = run(swa_pe(QTs, QTsn, TS, [(K2C, "K2C", Vc, Vcn, 128), (K2n_, K2nn, Vm, Vmn, TS)], False))
        swa_post(Os, Osn, TS, sgs_s, sgs_sn, mixt, mixtn)
        cached_mem_kv(cmk[1, bb], cmv[1, bb], mkTs, "mkTs", mvas, "mvas")
        Om, Omn = run(mem_pe(mqTs, mqTsn, TS, mkTs, "mkTs", mvas, "mvas"))
        mem_post(Om, Omn, TS, sgm_s, sgm_sn, mixt, mixtn)
        em.op("dve", lambda e, r0=r0, mixt=mixt: e.tensor_copy(mixs[r0:r0 + 32, :], mixt[r0:r0 + 32, :]), reads=[mixtn], writes=["mixs"])
    run(merge_ln(mixs, "mixs", TS, 2048, xs1, "xs1", 1, xs1, "xs1"))
    em.dma(y_s, xs1[0:TS, :], reads=["xs1"])

    em.dma(mkTs[:].rearrange("p a b -> p (a b)"), mkT1_scr, reads=["mkT1_scr"], writes=["mkTs"])
    em.dma(mvas[:].rearrange("p a b c -> p (a b c)"), mva1_scr, reads=["mva1_scr"], writes=["mvas"])
    ctx2 = {}
    ring = {}
    xts = {}

    xbs2 = {}
    hbs2 = {}

    def load2(t):
        xb, xbn = xbR.next()
        em.dma(xb[:], scr[(t - T0) * 128:(t - T0 + 1) * 128, :], reads=[("scr", t)], writes=[xbn], eng="pool")
        xbs2[t] = (xb, xbn)

    def loadh2(t):
        hb, hbn = hR.next()
        em.dma(hb[:], scr[(t - T0) * 128:(t - T0 + 1) * 128, :], reads=[("scr", t)], writes=[hbn])
        hbs2[t] = (hb, hbn)

    def front2a(t):
        xb, xbn = xbs2.pop(t)
        xT, xTn = make_xT(xb, xbn, 128, rot=xT2R)
        xts[t] = (xT, xTn)
        yield
        last = (t == NT - 1)
        cur, _, _ = yield from shared_kv(xT, xTn, 128, cosP[:, t, :], sinP[:, t, :], ["cosP", "sinP"],
                                         swk_p if last else None, swv_p if last else None,
                                         pflag[:, 0:1] if t == T0 else None)
        ring[t] = cur

    def front2b(t):
        xT, xTn = xts.pop(t)
        ctx2[t] = yield from layer_front(xT, xTn, 128, cosP[:, t, :], sinP[:, t, :], ["cosP", "sinP"])

    def attn2(t):
        QT, QTn, sgs, sgsn, mqT, mqTn, sgm, sgmn = ctx2[t]
        Os, Osn = yield from swa_pe(QT, QTn, 128, [ring[t - 1], ring[t]], True)
        Om, Omn = yield from mem_pe(mqT, mqTn, 128, mkTs, "mkTs", mvas, "mvas")
        mix, mixn = mixPR.next()
        swa_post(Os, Osn, 128, sgs, sgsn, mix, mixn)
        yield
        mem_post(Om, Omn, 128, sgm, sgmn, mix, mixn)
        ctx2[t] = (mix, mixn)
        ring.pop(t - 1)
        yield

    def back2(t):
        mix, mixn = ctx2.pop(t)
        hb, hbn = hbs2.pop(t)
        yield from merge_ln(mix, mixn, 128, 2048, hb, hbn, 1, hb, hbn, rot=mixT2R)
        em.dma(y_p[(t - OWN0) * 128:(t - OWN0 + 1) * 128, :], hb[:], reads=[hbn])
        yield

    CP[0] = "act"
    MIXENG[0] = "pool"
    LNENG[0] = "pool"
    S_SINGLE[0] = True
    P_POOL[0] = P_SMALL + [(psB[:, 0:512], "B0"), (psB[:, 512:1024], "B1")]
    load2(T0)
    load2(OWN0)
    weave(front2a(T0), front2a(OWN0))
    xts.pop(T0)
    load2(OWN0 + 1)
    load2(OWN0 + 2)
    weave(front2a(OWN0 + 1), front2b(OWN0))
    for t in range(OWN0, NT):
        if t + 3 < NT:
            load2(t + 3)
        loadh2(t)
        weave(attn2(t), delayed(back2(t - 1), 2) if t > OWN0 else None,
              front2b(t + 1) if t + 1 < NT else None, front2a(t + 2) if t + 2 < NT else None)
    run(back2(NT - 1))

    stats = em.finalize()
    return nc, stats


_CACHE = {}


def _rope_tab(pos):
    inv = (np.float32(500000.0) ** (-np.arange(8, dtype=np.float32) / np.float32(8))).astype(np.float32)
    ang = pos.astype(np.float32)[:, None] * inv[None, :]
    return np.cos(ang).astype(np.float32), np.sin(ang).astype(np.float32)


def kernel(x_prompt, x_sample, mem_prompt, cache_diff_k, cache_diff_v, cache_swa_k, cache_swa_v,
           cache_mem_k, cache_mem_v, w_in_a, lam_q1, lam_k1, lam_q2, lam_k2, diff_norm_g,
           w_in_b, sinks, w_kv_shared, w_mem_kv, w_out, ln_g, ln_b):
    f = lambda a: np.ascontiguousarray(np.asarray(a), dtype=np.float32)
    x_prompt, x_sample, mem_prompt = f(x_prompt), f(x_sample), f(mem_prompt)
    cache_diff_k, cache_diff_v = f(cache_diff_k), f(cache_diff_v)
    cache_swa_k, cache_swa_v = f(cache_swa_k), f(cache_swa_v)
    cache_mem_k, cache_mem_v = f(cache_mem_k), f(cache_mem_v)
    if "nc" not in _CACHE:
        _CACHE["nc"], _CACHE["stats"] = build_program()
    nc = _CACHE["nc"]
    ident = np.eye(128, dtype=np.float32).astype(ml_dtypes.bfloat16)
    lamv = np.concatenate([f(lam_q1), f(lam_k1), f(lam_q2), f(lam_k2)], axis=0)
    spos = np.full(TS, 4096, dtype=np.int64)
    for bb in range(2):
        spos[bb * 32: bb * 32 + 16] = 4096 + np.arange(16)
    cosS, sinS = _rope_tab(spos)
    in_maps = []
    for c in range(8):
        b, j = c // 2, c % 2
        xl = np.zeros((NT * 128, 1024), np.float32)
        if j == 1:
            xl[:] = x_prompt[b]
        else:
            xl[2048:] = x_prompt[b, 0:2048]
        pos = np.maximum(np.arange(NT * 128) - 2048 * (1 - j), 0)
        cp, sp = _rope_tab(pos)
        cosP = np.ascontiguousarray(cp.reshape(NT, 128, 8).transpose(1, 0, 2))
        sinP = np.ascontiguousarray(sp.reshape(NT, 128, 8).transpose(1, 0, 2))
        xs = np.zeros((TS, 1024), np.float32)
        for bb in range(2):
            xs[bb * 32: bb * 32 + 16] = x_sample[2 * c + bb]
        in_maps.append({
            "xloc": xl, "xs": xs, "memp": mem_prompt[b],
            "cdk": cache_diff_k[0, 2 * c:2 * c + 2].reshape(2, 4096, 512),
            "cdv": cache_diff_v[0, 2 * c:2 * c + 2].reshape(2, 4096, 512),
            "cswk": cache_swa_k[2 * c:2 * c + 2].reshape(2, 128, 128),
            "cswv": cache_swa_v[2 * c:2 * c + 2].reshape(2, 128, 128),
            "cmk": cache_mem_k[:, 2 * c:2 * c + 2].reshape(2, 2, 256, 512),
            "cmv": cache_mem_v[:, 2 * c:2 * c + 2].reshape(2, 2, 256, 512),
            "w_in_a": f(w_in_a)[0], "w_in_b": f(w_in_b)[0], "w_kv": f(w_kv_shared),
            "w_mem": f(w_mem_kv), "w_out": f(w_out), "lamv": lamv, "dng": f(diff_norm_g),
            "sinks": f(sinks), "lng": f(ln_g), "lnb": f(ln_b),
            "cosP": cosP, "sinP": sinP, "cosS": cosS, "sinS": sinS,
            "pflag": np.full((128, 1), float(j), np.float32), "ident": ident,
        })
    res = run_bass_kernel_spmd(nc, in_maps, core_ids=list(range(8)))
    R = [{k: np.asarray(v) for k, v in r.items()} for r in res.results]
    y_prompt = np.zeros((4, 4096, 1024), np.float32)
    dkp = np.zeros((1, 4, 4096, 4, 2, 64), np.float32)
    dvp = np.zeros((1, 4, 4096, 4, 128), np.float32)
    y_sample = np.zeros((16, 16, 1024), np.float32)
    dks = np.zeros((1, 16, 16, 4, 2, 64), np.float32)
    dvs = np.zeros((1, 16, 16, 4, 128), np.float32)
    swkp = np.zeros((4, 128, 2, 64), np.float32)
    swvp = np.zeros((4, 128, 2, 64), np.float32)
    swks = np.zeros((16, 128, 2, 64), np.float32)
    swvs = np.zeros((16, 128, 2, 64), np.float32)
    mkp = np.zeros((2, 4, 256, 4, 128), np.float32)
    mvp = np.zeros((2, 4, 256, 4, 128), np.float32)
    for c in range(8):
        b, j = c // 2, c % 2
        r = R[c]
        sl = slice(j * 2048, (j + 1) * 2048)
        y_prompt[b, sl] = r["y_p"]
        dkp[0, b, sl] = r["dk_p"].reshape(2048, 4, 2, 64)
        dvp[0, b, sl] = r["dv_p"].reshape(2048, 4, 128)
        for bb in range(2):
            rows = slice(bb * 32, bb * 32 + 16)
            y_sample[2 * c + bb] = r["y_s"][rows]
            dks[0, 2 * c + bb] = r["dk_s"][rows].reshape(16, 4, 2, 64)
            dvs[0, 2 * c + bb] = r["dv_s"][rows].reshape(16, 4, 128)
            swks[2 * c + bb] = r["swk_s"][bb].reshape(128, 2, 64)
            swvs[2 * c + bb] = r["swv_s"][bb].reshape(128, 2, 64)
        if j == 1:
            swkp[b] = r["swk_p"].reshape(128, 2, 64)
            swvp[b] = r["swv_p"].reshape(128, 2, 64)
        else:
            mkp[:, b] = r["mk_p"].reshape(2, 256, 4, 128)
            mvp[:, b] = r["mv_p"].reshape(2, 256, 4, 128)
    return (y_prompt, y_sample, dkp, dvp, dks, dvs, swkp, swvp, swks, swvs, mkp, mvp)
```
